# Optimizing a Trainium2 kernel written in Bass

```python
import math
import jax, jax.numpy as jnp
from jax import lax
import numpy as np

D_MODEL = 1024
BATCH = 2
SEQ = 8192
DEPTH = 4

N_MIXERS = 3
N_META = 16
RMS_EPS = 1e-6
N_HEADS = 16
N_KV_HEADS = 4
HEAD_DIM = 64
GROUP = N_HEADS // N_KV_HEADS
WINDOW = 128
BLOCK = 128
N_BUCKETS = 32
MAX_DISTANCE = 128
CONV_WIDTH = 3
POOL_WINDOWS = (2, 4, 8, 16)
N_POOL_GROUPS = len(POOL_WINDOWS)
POOL_GROUP_DIM = D_MODEL // N_POOL_GROUPS
D_FF = ((8 * D_MODEL + 3 * 256 - 1) // (3 * 256)) * 256
N_ATTN = len(range(0, DEPTH, N_MIXERS))
N_CONV = len(range(1, DEPTH, N_MIXERS))
N_POOL = len(range(2, DEPTH, N_MIXERS))

kernel_name = "hybrid_swa_sink_shortconv_pool_decoder"


def rms_norm(x, g):
    xf = x.astype(jnp.float32)
    y = xf * lax.rsqrt(jnp.mean(xf * xf, axis=-1, keepdims=True) + RMS_EPS)
    return (y * g.astype(jnp.float32)).astype(x.dtype)


def rel_bucket(dist):
    max_exact = N_BUCKETS // 2
    d = jnp.maximum(dist, 0)
    df = jnp.maximum(d, 1).astype(jnp.float32)
    large = max_exact + (jnp.log(df / max_exact) / math.log(MAX_DISTANCE / max_exact)
                         * (N_BUCKETS - max_exact)).astype(jnp.int32)
    large = jnp.minimum(large, N_BUCKETS - 1)
    return jnp.where(d < max_exact, d, large)


def rel_bias(rel_table, dist):
    b = rel_table.astype(jnp.float32)[rel_bucket(dist)]
    return jnp.moveaxis(b, -1, 0)


def sliding_window_attention(h, w_qkv, b_qkv, w_o, b_o, sinks, rel_table):
    bsz, L, _ = h.shape
    S = L - N_META
    nb = S // BLOCK
    qkv = h @ w_qkv + b_qkv
    q, k, v = jnp.split(qkv, [N_HEADS * HEAD_DIM, (N_HEADS + N_KV_HEADS) * HEAD_DIM], axis=-1)
    q = q.reshape(bsz, L, N_KV_HEADS, GROUP, HEAD_DIM) * (HEAD_DIM ** -0.5)
    k = k.reshape(bsz, L, N_KV_HEADS, HEAD_DIM)
    v = v.reshape(bsz, L, N_KV_HEADS, HEAD_DIM)
    qm, qr = q[:, :N_META], q[:, N_META:]
    km, kr = k[:, :N_META], k[:, N_META:]
    vm, vr = v[:, :N_META], v[:, N_META:]
    sink = sinks.astype(jnp.float32).reshape(N_KV_HEADS, GROUP)

    im = jnp.arange(N_META)
    dist_mm = im[:, None] - im[None, :]
    s_mm = jnp.einsum('bqkgd,bmkd->bkgqm', qm, km).astype(jnp.float32)
    s_mm = s_mm + rel_bias(rel_table, dist_mm).reshape(N_KV_HEADS, GROUP, N_META, N_META)
    s_mm = jnp.where(dist_mm >= 0, s_mm, -jnp.inf)
    sink_mm = jnp.broadcast_to(sink[None, :, :, None, None], s_mm.shape[:-1] + (1,))
    p_mm = jax.nn.softmax(jnp.concatenate([s_mm, sink_mm], axis=-1), axis=-1)[..., :N_META]
    o_m = jnp.einsum('bkgqm,bmkd->bqkgd', p_mm.astype(v.dtype), vm)
    o_m = o_m.reshape(bsz, N_META, N_HEADS * HEAD_DIM)

    qb = qr.reshape(bsz, nb, BLOCK, N_KV_HEADS, GROUP, HEAD_DIM)
    kb = jnp.pad(kr, ((0, 0), (BLOCK, 0), (0, 0), (0, 0))).reshape(bsz, nb + 1, BLOCK, N_KV_HEADS, HEAD_DIM)
    vb = jnp.pad(vr, ((0, 0), (BLOCK, 0), (0, 0), (0, 0))).reshape(bsz, nb + 1, BLOCK, N_KV_HEADS, HEAD_DIM)
    k_band = jnp.concatenate([kb[:, :-1], kb[:, 1:]], axis=2)
    v_band = jnp.concatenate([vb[:, :-1], vb[:, 1:]], axis=2)

    iq = jnp.arange(BLOCK)[:, None]
    jk = jnp.arange(2 * BLOCK)[None, :]
    dist_band = BLOCK + iq - jk
    blk = jnp.arange(nb)[:, None, None]
    valid = (dist_band >= 0) & (dist_band < WINDOW) & ((blk > 0) | (jk >= BLOCK))
    bias_band = rel_bias(rel_table, dist_band).reshape(N_KV_HEADS, GROUP, BLOCK, 2 * BLOCK)

    qpos = N_META + jnp.arange(nb)[:, None] * BLOCK + jnp.arange(BLOCK)[None, :]
    dist_meta = qpos[:, :, None] - im[None, None, :]
    bias_meta = jnp.moveaxis(rel_bias(rel_table, dist_meta), 0, 1)
    bias_meta = bias_meta.reshape(nb, N_KV_HEADS, GROUP, BLOCK, N_META)

    s_band = jnp.einsum('bnqkgd,bnskd->bnkgqs', qb, k_band).astype(jnp.float32) + bias_band
    s_band = jnp.where(valid[None, :, None, None], s_band, -jnp.inf)
    s_meta = jnp.einsum('bnqkgd,bmkd->bnkgqm', qb, km).astype(jnp.float32) + bias_meta[None]
    sink_b = jnp.broadcast_to(sink[None, None, :, :, None, None], s_band.shape[:-1] + (1,))
    p = jax.nn.softmax(jnp.concatenate([s_meta, s_band, sink_b], axis=-1), axis=-1)
    p_meta = p[..., :N_META].astype(v.dtype)
    p_band = p[..., N_META:N_META + 2 * BLOCK].astype(v.dtype)
    o_r = (jnp.einsum('bnkgqm,bmkd->bnqkgd', p_meta, vm)
           + jnp.einsum('bnkgqs,bnskd->bnqkgd', p_band, v_band))
    o_r = o_r.reshape(bsz, S, N_HEADS * HEAD_DIM)

    o = jnp.concatenate([o_m, o_r], axis=1)
    return o @ w_o + b_o


def short_conv_mixer(h, w_in, conv_w, w_out):
    L = h.shape[1]
    gate_b, gate_c, u = jnp.split(h @ w_in, 3, axis=-1)
    z = gate_c * u
    zp = jnp.pad(z, ((0, 0), (CONV_WIDTH - 1, 0), (0, 0)))
    conv = sum(conv_w[t] * zp[:, t:t + L] for t in range(CONV_WIDTH))
    return (gate_b * conv) @ w_out


def pooling_mixer(h, w_pool, scale):
    bsz, L, D = h.shape
    hf = h.astype(jnp.float32).reshape(bsz, L, N_POOL_GROUPS, POOL_GROUP_DIM)
    cs = jnp.pad(lax.cumsum(hf, axis=1), ((0, 0), (1, 0), (0, 0), (0, 0)))
    t = jnp.arange(L)[:, None]
    win = jnp.array(POOL_WINDOWS, dtype=jnp.int32)[None, :]
    lo = jnp.maximum(t + 1 - win, 0)
    count = jnp.minimum(win, t + 1).astype(jnp.float32)
    lower = cs[:, lo, jnp.arange(N_POOL_GROUPS)[None, :], :]
    mix = (cs[:, 1:] - lower) / count[None, :, :, None] - hf
    out = jnp.einsum('blgc,gcd->blgd', mix.astype(h.dtype), w_pool).reshape(bsz, L, D)
    return out * scale


def swiglu(h, w_gate, w_up, w_down):
    return (jax.nn.silu(h @ w_gate) * (h @ w_up)) @ w_down


def setup_inputs(seed: int = 0) -> dict:
    key = jax.random.key(seed)
    ks = jax.random.split(key, 20)
    f32 = jnp.float32
    qkv_out = (N_HEADS + 2 * N_KV_HEADS) * HEAD_DIM
    nrm = lambda k, shape, s: jax.random.normal(k, shape, f32) * s
    return {
        "x": nrm(ks[0], (BATCH, SEQ, D_MODEL), 1.0),
        "meta_tokens": nrm(ks[1], (N_META, D_MODEL), 1.0),
        "rel_bias_table": nrm(ks[2], (N_BUCKETS, N_HEADS), 0.5),
        "norm_mix": 1.0 + nrm(ks[3], (DEPTH, D_MODEL), 0.02),
        "norm_ffn": 1.0 + nrm(ks[4], (DEPTH, D_MODEL), 0.02),
        "norm_final": 1.0 + nrm(ks[5], (D_MODEL,), 0.02),
        "attn_w_qkv": nrm(ks[6], (N_ATTN, D_MODEL, qkv_out), D_MODEL ** -0.5),
        "attn_b_qkv": nrm(ks[7], (N_ATTN, qkv_out), 0.02),
        "attn_w_o": nrm(ks[8], (N_ATTN, N_HEADS * HEAD_DIM, D_MODEL), (N_HEADS * HEAD_DIM) ** -0.5),
        "attn_b_o": nrm(ks[9], (N_ATTN, D_MODEL), 0.02),
        "attn_sinks": nrm(ks[10], (N_ATTN, N_HEADS), 1.0),
        "conv_w_in": nrm(ks[11], (N_CONV, D_MODEL, 3 * D_MODEL), D_MODEL ** -0.5),
        "conv_w": nrm(ks[12], (N_CONV, CONV_WIDTH, D_MODEL), CONV_WIDTH ** -0.5),
        "conv_w_out": nrm(ks[13], (N_CONV, D_MODEL, D_MODEL), D_MODEL ** -0.5),
        "pool_w": nrm(ks[14], (N_POOL, N_POOL_GROUPS, POOL_GROUP_DIM, POOL_GROUP_DIM), POOL_GROUP_DIM ** -0.5),
        "pool_scale": 1.0 + nrm(ks[15], (N_POOL, D_MODEL), 0.1),
        "ffn_w_gate": nrm(ks[16], (DEPTH, D_MODEL, D_FF), D_MODEL ** -0.5),
        "ffn_w_up": nrm(ks[17], (DEPTH, D_MODEL, D_FF), D_MODEL ** -0.5),
        "ffn_w_down": nrm(ks[18], (DEPTH, D_FF, D_MODEL), D_FF ** -0.5),
    }


def reference(x, meta_tokens, rel_bias_table, norm_mix, norm_ffn, norm_final,
              attn_w_qkv, attn_b_qkv, attn_w_o, attn_b_o, attn_sinks,
              conv_w_in, conv_w, conv_w_out,
              pool_w, pool_scale,
              ffn_w_gate, ffn_w_up, ffn_w_down):
    bsz = x.shape[0]
    meta = jnp.broadcast_to(meta_tokens.astype(x.dtype)[None], (bsz, N_META, D_MODEL))
    h = jnp.concatenate([meta, x], axis=1)
    for i in range(DEPTH):
        kind, j = i % N_MIXERS, i // N_MIXERS
        a = rms_norm(h, norm_mix[i])
        if kind == 0:
            m = sliding_window_attention(a, attn_w_qkv[j], attn_b_qkv[j], attn_w_o[j], attn_b_o[j],
                                         attn_sinks[j], rel_bias_table)
        elif kind == 1:
            m = short_conv_mixer(a, conv_w_in[j], conv_w[j], conv_w_out[j])
        else:
            m = pooling_mixer(a, pool_w[j], pool_scale[j])
        h = h + m.astype(h.dtype)
        h = h + swiglu(rms_norm(h, norm_ffn[i]), ffn_w_gate[i], ffn_w_up[i], ffn_w_down[i])
    h = rms_norm(h, norm_final)
    return h[:, N_META:]
```

```python
import os
import numpy as np
import concourse.bass as bass
import concourse.mybir as mybir
from concourse.bass_utils import run_bass_kernel_spmd

F32 = mybir.dt.float32
BF16 = mybir.dt.bfloat16
AF = mybir.ActivationFunctionType
ALU = mybir.AluOpType

D = 1024
KC = 8
DFF = 2816
FC = 22
NMETA = 16
NHALO = 3
EPS = 1e-6
N_CORES = 8
SEQ = 8192
CHUNK = 2048
POOL_W = (2, 2, 4, 4, 8, 8, 16, 16)
SLOT_ORDER = (0, 2, 1, 3)
RING_ELEMS = 2048
NSLOT = 5


class _Rec:
    def __getattr__(self, name):
        def f(*a, **kw):
            return (name, a, kw)
        return f


_I = _Rec()


class Buf:
    __slots__ = ("lw", "rd")

    def __init__(self):
        self.lw = None
        self.rd = []


class Prog:
    ENGS = ("pe", "act", "dve", "pool", "sp")

    def __init__(self):
        self.q = {e: [] for e in self.ENGS}
        self.cnt = {e: 0 for e in self.ENGS}
        self.seen = {e: {} for e in self.ENGS}
        self.bufs = {}

    def buf(self, *key):
        b = self.bufs.get(key)
        if b is None:
            b = self.bufs[key] = Buf()
        return b

    def op(self, eng, fn, reads=(), writes=(), inc=True, dma=None):
        raw = {}
        war = {}

        def need(d, tok):
            if tok is None:
                return
            k, v = tok
            if d.get(k, 0) < v:
                d[k] = v

        for b in reads:
            need(raw, b.lw)
        for b in writes:
            need(raw, b.lw)
            for r in b.rd:
                need(war, r)
        waits = []
        seen = self.seen[eng]
        for d, is_war in ((raw, False), (war, True)):
            for k, v in d.items():
                if k == eng and (eng == "pe" or is_war or dma is not None):
                    continue
                if seen.get(k, 0) < v:
                    seen[k] = v
                    waits.append((k, v))
        if dma is not None:
            self.cnt[dma] = self.cnt.get(dma, 0) + 16
            tok = (dma, self.cnt[dma])
            incinfo = (dma, 16)
        elif inc:
            self.cnt[eng] += 1
            tok = (eng, self.cnt[eng])
            incinfo = (eng, 1)
        else:
            tok = (eng, self.cnt[eng] + 1)
            incinfo = None
        for b in reads:
            b.rd.append(tok)
        for b in writes:
            b.lw = tok
            b.rd = []
        self.q[eng].append((waits, fn, incinfo))
        return tok

    def alias(self, old_bufs, new_bufs):
        toks = []
        for o in old_bufs:
            if o.lw is not None:
                toks.append(o.lw)
            toks.extend(o.rd)
        for n in new_bufs:
            n.rd.extend(toks)

    def emit(self, nc, final_waits):
        sems = {}
        for k in list(self.cnt.keys()):
            sems[k] = nc.alloc_semaphore("s_" + str(k).replace(" ", "").replace("'", "").replace("(", "").replace(")", "").replace(",", "_"))
        qmap = {"pe": "tensor", "act": "scalar", "dve": "vector", "pool": "gpsimd", "sp": "sync"}
        with nc.Block() as block:
            for eng in self.ENGS:
                lst = self.q[eng]
                fw = final_waits if eng == "sp" else ()

                def body(e, lst=lst, fw=fw):
                    for waits, fn, incinfo in lst:
                        for k, v in waits:
                            e.wait_ge(sems[k], v)
                        name, a_, kw_ = fn
                        ins = getattr(e, name)(*a_, **kw_)
                        if incinfo is not None:
                            ins.then_inc(sems[incinfo[0]], incinfo[1])
                    for k in fw:
                        e.wait_ge(sems[k], self.cnt[k])

                getattr(block, qmap[eng])(body)


def _arr(W):
    K, M = W.shape
    return np.ascontiguousarray(W.reshape(K // 128, 128, M // 128, 128).transpose(2, 1, 0, 3))


def _fm(v):
    lead = v.shape[:-1]
    r = v.reshape(lead + (KC, 128))
    r = np.moveaxis(r, -1, 0)
    return np.ascontiguousarray(r)


def _rel_bucket_np(d):
    d = np.maximum(d, 0)
    df = np.maximum(d, 1).astype(np.float32)
    large = 16 + (np.log(df / np.float32(16)) / np.float32(np.log(128 / 16)) * np.float32(16)).astype(np.int32)
    large = np.minimum(large, 31)
    return np.where(d < 16, d, large)


def _bucket_table():
    return _rel_bucket_np(np.arange(0, 512))


def _passes(q0, nb):
    blocks = list(range(q0, nb))
    n = len(blocks)
    npass = 3 if n >= 3 else 1
    base, rem = divmod(n, npass)
    out = []
    i = 0
    for p in range(npass):
        c = base + (1 if p < rem else 0)
        out.append(blocks[i:i + c])
        i += c
    return out


def _split_tiles(nblk):
    nt = (nblk + 3) // 4
    base, rem = divmod(nblk, nt)
    return [base + (1 if i < rem else 0) for i in range(nt)]


def build(n_own=16, n_layers=4, debug_h=False):
    NB = NHALO + n_own
    T = NB * 128
    nc = bass.Bass("TRN2", target_bir_lowering=False)
    P = Prog()
    B = P.buf

    def din(name, shape):
        return nc.dram_tensor(name, list(shape), F32, kind="ExternalInput").ap()

    x_d = din("x", [T, D])
    meta_d = din("meta", [NMETA, D])
    ident_d = din("ident", [128, 128])
    flag_d = din("flags", [128, 2])
    gam_d = din("gammas", [128, 9 * KC])
    gu_d = din("ffn_gu", [4, FC, 128, 2 * KC * 128])
    dn_d = din("ffn_dn", [4, 3, KC, 128, 8 * 128])
    aq_d = din("attn_q", [2, 8, 128, KC * 128])
    ak_d = din("attn_k", [2, 8, 128, KC * 128])
    av_d = din("attn_v", [2, 128, KC * 256])
    ao_d = din("attn_o", [2, 8, 128, KC * 128])
    ab_d = din("attn_b", [2, 128, 24])
    abv_d = din("attn_bv", [2, 128, 256])
    sink_d = din("attn_sink", [2, 16])
    ebp_d = din("bias_prev", [128, 4 * 512])
    ebc_d = din("bias_cur", [128, 4 * 512])
    mkp_d = din("mask_prev", [128, 4 * 512])
    mkc_d = din("mask_cur", [128, 4 * 512])
    bmf_d = din("bias_meta_first", [NMETA, 4 * 512])
    bmr_d = din("bias_meta_rest", [NMETA, 16])
    bmm_d = din("bias_mm", [NMETA, 4 * 64])
    mmm_d = din("mask_mm", [NMETA, 4 * 64])
    ci_d = din("conv_in", [8, 3, 128, KC * 128])
    co_d = din("conv_out", [8, 128, KC * 128])
    cw_d = din("conv_w", [128, 3 * KC])
    pw_d = din("pool_w", [128, 4 * 2 * 256])
    psc_d = din("pool_scale", [128, KC])
    pic_d = din("pool_invc", [128, KC * NMETA])
    out_rows = n_own * 128 + (128 if debug_h else 0)
    out_d = nc.dram_tensor("out", [out_rows, D], F32, kind="ExternalOutput").ap()

    def sb(name, shape, dt=F32):
        return nc.alloc_sbuf_tensor("sb_" + name, list(shape), dt)

    h = sb("h", [128, KC, T])
    hm = sb("hm", [128, KC, NMETA])
    maxpass = max(len(p) for q0 in (1, 2, 3) for p in _passes(min(q0, NB - 1), NB))
    NTP = max(NMETA + 128 + maxpass * 128, 768)
    abf_raw = sb("abf", [128, KC * NTP], BF16)
    abf = abf_raw[:].rearrange("p (k t) -> p k t", k=KC)
    Et = abf_raw[:, 0:4096].bitcast(F32).rearrange("p (a c w) -> p a c w", a=2, c=2)
    Em = abf_raw[:, 4096:6144].bitcast(F32).rearrange("p (a w) -> p a w", a=2)
    gbf = sb("gbf", [128, 8, NTP], BF16)
    qt = gbf
    kt = sb("kt", [128, 8, NTP], BF16)
    vx = sb("vx", [128, maxpass + 1, 4 * 128], BF16)
    vxm = sb("vxm", [64, 4 * 128], BF16)
    ring = sb("ring", [128, NSLOT, RING_ELEMS], BF16)
    ident = sb("ident", [128, 128])
    onesb = sb("onesb", [128, 128], BF16)
    flags = sb("flags", [128, 2])
    gam = sb("gam", [128, 9 * KC])
    epsb = sb("epsb", [128, 1])
    ebp = sb("ebp", [128, 4 * 512])
    ebc = sb("ebc", [128, 4 * 512])
    ebmr = sb("ebmr", [NMETA, 16])
    ebmm = sb("ebmm", [NMETA, 4 * 64])
    attb = sb("attb", [128, 2, 24])
    attbv = sb("attbv", [128, 2, 256])
    sinkst = sb("sinkst", [64, 2 * 16])
    cw = sb("cw", [128, 3 * KC])
    psc = sb("psc", [128, KC])
    pic = sb("pic", [128, KC * NMETA])
    sq = sb("sq", [128, 2, 512], BF16)
    rstd = sb("rstd", [128, 2, 512])
    sgt = sb("sgt", [128, 2, 512])
    rden = sgt
    SCR = 3072
    scr = sb("scr", [128, SCR])
    stage = scr[:, 1024:3072].rearrange("p (a w) -> p a w", a=2)
    mstage = scr[:, 0:2048]
    ebmf = scr[0:NMETA, 0:2048]
    zt = scr[:, 0:1028].rearrange("p (a w) -> p a w", a=2)
    cacc = scr[:, 1028:2052].rearrange("p (a w) -> p a w", a=2)
    afk = scr[:, 0:1056].rearrange("p (a w) -> p a w", a=2)
    pls = scr[:, 1056:2640].rearrange("p (a w) -> p a w", a=3)
    yf = scr[:, 0:1024].rearrange("p (k w) -> p k w", k=KC)
    Pt = sb("Pt", [128, 2, 2, 512], BF16)
    Pm = sb("Pm", [64, 4, 512], BF16)
    Pmm = sb("Pmm", [64, 4, 64], BF16)
    zcar = sb("zcar", [128, KC, 2])
    acar = sb("acar", [128, KC, NMETA])
    tmpb = sb("tmpb", [128, KC, NMETA])
    ps = [nc.alloc_psum_tensor("ps%d" % i, [128, 512], F32) for i in range(8)]
    psb = [B("ps", i) for i in range(8)]

    gcol = {"mix": lambda l: gam[:, l * KC:(l + 1) * KC], "ffn": lambda l: gam[:, (4 + l) * KC:(5 + l) * KC],
            "fin": lambda l: gam[:, 8 * KC:9 * KC]}

    scr_owner = [[]]

    def scr_claim(bufs):
        P.alias(scr_owner[0], bufs)
        scr_owner[0] = list(bufs)

    ABF_ALL = [B("abf", "m"), B("abf", "L")] + [B("abf", i) for i in range(maxpass)]
    ET_ALL = [B("Et", a_, c_) for a_ in range(2) for c_ in range(2)] + [B("Em", 0), B("Em", 1)]

    setup_bufs = []

    def sload(dst, src, *key):
        b = B(*key)
        setup_bufs.append(b)
        P.op("sp", _I.dma_start(out=dst, in_=src), writes=[b], dma="setup")
        return b

    sload(ident[:], ident_d, "ident")
    sload(flags[:], flag_d, "flags")
    sload(gam[:], gam_d, "gam")
    sload(ebp[:], ebp_d, "ebp")
    sload(ebc[:], ebc_d, "ebc")
    sload(ebmr[:], bmr_d, "ebmr")
    sload(ebmm[:], bmm_d, "ebmm")
    sload(tmpb[0:NMETA, :, :].rearrange("p a b -> p (a b)")[:, 0:128], mmm_d[:, 0:128], "tmpb")
    for l in range(2):
        sload(attb[:, l, :], ab_d[l], "attb", l)
        sload(attbv[:, l, :], abv_d[l], "attbv", l)
        sload(sinkst[32:33, l * 16:(l + 1) * 16], sink_d[l:l + 1, :], "sinkst", l)
    sload(cw[:], cw_d, "cw")
    sload(psc[:], psc_d, "psc")
    sload(pic[:], pic_d, "pic")
    tot = P.cnt["setup"]
    for b in setup_bufs:
        b.lw = ("setup", tot)

    P.op("dve", _I.memset(onesb[:], 1.0 / 1024.0), writes=[B("onesb")])
    P.op("dve", _I.memset(epsb[:], EPS), writes=[B("epsb")])
    for t_, key, md in ((ebp, "ebp", mkp_d), (ebc, "ebc", mkc_d)):
        scr_claim([B("mstage", key)])
        P.op("sp", _I.dma_start(out=mstage, in_=md), writes=[B("mstage", key)], dma=("mst", key))
        P.op("act", _I.activation(out=t_[:], in_=t_[:], func=AF.Exp), reads=[B(key)], writes=[B(key)])
        P.op("dve", _I.tensor_tensor(out=t_[:], in0=t_[:], in1=mstage, op=ALU.mult),
             reads=[B(key), B("mstage", key)], writes=[B(key)])
    for t_, key in ((ebmr, "ebmr"), (ebmm, "ebmm")):
        P.op("act", _I.activation(out=t_[:], in_=t_[:], func=AF.Exp), reads=[B(key)], writes=[B(key)])
    mm4 = ebmm[:].rearrange("p (a q) -> p a q", q=NMETA)
    msk = tmpb[0:NMETA, :, :].rearrange("p a b -> p (a b)")[:, 0:NMETA]
    msk_b = bass.AP(msk.tensor, msk.offset, [list(msk.ap[0]), [0, 16], [1, NMETA]])
    P.op("dve", _I.tensor_tensor(out=mm4, in0=mm4, in1=msk_b, op=ALU.mult),
         reads=[B("ebmm"), B("tmpb")], writes=[B("ebmm")])
    P.op("dve", _I.memset(zcar[:], 0.0), writes=[B("zcar")])
    P.op("dve", _I.memset(acar[:], 0.0), writes=[B("acar")])
    vxm3 = vxm[:].rearrange("p (a b) -> p a b", a=4)
    P.op("dve", _I.memset(vxm[:], 0.0), writes=[B("vx", "m")])
    P.op("dve", _I.memset(vxm3[32:33, :, 64:128], 1.0), writes=[B("vx", "m")])
    P.op("dve", _I.memset(vxm3[0:NMETA, :, 64:128], 1.0), writes=[B("vx", "m")])
    vx4 = vx[:].rearrange("p s (a b) -> p s a b", a=4)
    P.op("dve", _I.memset(vx[:], 1.0), writes=[B("vx", "L")] + [B("vx", s_) for s_ in range(maxpass)])
    P.op("dve", _I.memset(Pm[:], 0.0), writes=[B("Pm", kv) for kv in range(4)])
    P.op("dve", _I.memset(Pmm[:], 0.0), writes=[B("Pmm", kv) for kv in range(4)])

    xs_toggle = [0]
    scr_claim([B("stage", 0), B("stage", 1)])

    def load_tokens(src_ap, nrows, dst3, hbufs):
        i = xs_toggle[0] % 2
        xs_toggle[0] += 1
        P.op("sp", _I.dma_start(out=stage[0:nrows, i, :], in_=src_ap), writes=[B("stage", i)], dma=("xs", i))
        for half in range(2):
            bank = 2 * i + half
            for kk in range(4):
                k = half * 4 + kk
                P.op("pe", _I.transpose(
                    out=ps[bank][:, kk * 128:kk * 128 + nrows], in_=stage[0:nrows, i, k * 128:(k + 1) * 128],
                    identity=ident[0:nrows, 0:nrows]),
                    reads=[B("stage", i), B("ident")], writes=[psb[bank]], inc=(kk == 3))
            src = ps[bank][:].rearrange("p (a b) -> p a b", a=4)[:, :, 0:nrows]
            if half == 0:
                P.op("act", _I.copy(out=dst3[:, half * 4:half * 4 + 4, :], in_=src),
                     reads=[psb[bank]], writes=hbufs)
            else:
                P.op("dve", _I.tensor_copy(out=dst3[:, half * 4:half * 4 + 4, :], in_=src),
                     reads=[psb[bank]], writes=hbufs)

    load_tokens(meta_d, NMETA, hm[:], [B("hm")])
    for blk in range(NB):
        load_tokens(x_d[blk * 128:(blk + 1) * 128, :], 128, h[:, :, blk * 128:(blk + 1) * 128], [B("h", blk)])

    unit_ctr = [0]

    def load_unit(src_ap, nelem):
        u = unit_ctr[0]
        unit_ctr[0] += 1
        s = u % NSLOT
        dst = ring[:, s, 0:nelem]
        P.op("pool", _I.dma_start(out=dst, in_=src_ap), writes=[B("ring", s)], dma=("ring", s))
        return s

    class Tile:
        pass

    def scol(sl):
        return 0 if sl == "m" else (NMETA if sl == "L" else NMETA + 128 + sl * 128)

    def make_tiles(blocks, with_meta, left_block=None):
        tiles = []
        if with_meta:
            t = Tile()
            t.kind = "m"; t.c0 = 0; t.n = NMETA; t.blocks = []; t.slots = ["m"]
            t.hbufs = [B("hm")]
            t.hap = lambda k0, k1: hm[:, k0:k1, :]
            tiles.append(t)
        if left_block is not None:
            t = Tile()
            t.kind = "L"; t.c0 = NMETA; t.n = 128; t.blocks = [left_block]; t.slots = ["L"]
            t.hbufs = [B("h", left_block)]
            t.hap = lambda k0, k1, b=left_block: h[:, k0:k1, b * 128:(b + 1) * 128]
            tiles.append(t)
        pos = 0
        for nb_ in _split_tiles(len(blocks)):
            t = Tile()
            t.kind = "r"; t.c0 = NMETA + 128 + pos * 128; t.n = nb_ * 128
            t.blocks = blocks[pos:pos + nb_]; t.slots = list(range(pos, pos + nb_))
            t.hbufs = [B("h", b) for b in t.blocks]
            hc0 = t.blocks[0] * 128
            t.hap = lambda k0, k1, hc0=hc0, n=t.n: h[:, k0:k1, hc0:hc0 + n]
            tiles.append(t)
            pos += nb_
        for t in tiles:
            t.abufs = [B("abf", sl) for sl in t.slots]
            t.gb = lambda c, t=t: [B("gbf", c, sl) for sl in t.slots]
            t.ktb = [B("kt", sl) for sl in t.slots]
        return tiles

    sq_ctr = [0]
    nrm_ctr = [0]

    def rms_stats(hsrc, hb, n):
        ri = nrm_ctr[0] % 2
        nrm_ctr[0] += 1
        bank = 6 + ri
        for k in range(KC):
            si = sq_ctr[0] % 2
            sq_ctr[0] += 1
            P.op("act", _I.activation(out=sq[:, si, 0:n], in_=hsrc(k), func=AF.Square),
                 reads=hb, writes=[B("sq", si)])
            P.op("pe", _I.matmul(ps[bank][:, 0:n], lhsT=onesb[:], rhs=sq[:, si, 0:n],
                                                      start=(k == 0), stop=(k == KC - 1)),
                 reads=[B("sq", si), B("onesb")], writes=[psb[bank]], inc=True)
        P.op("act", _I.activation(out=rstd[:, ri, 0:n], in_=ps[bank][:, 0:n], func=AF.Sqrt, bias=epsb[:], scale=1.0),
             reads=[psb[bank], B("epsb")], writes=[B("rstd", ri)])
        P.op("dve", _I.reciprocal(out=rstd[:, ri, 0:n], in_=rstd[:, ri, 0:n]),
             reads=[B("rstd", ri)], writes=[B("rstd", ri)])
        return ri

    def tile_hsrc(t):
        return (lambda k: t.hap(k, k + 1)[:, 0, :]), t.hbufs, t.n

    def norm_to_abf(t, gcolap):
        hsrc, hb, n = tile_hsrc(t)
        ri = rms_stats(hsrc, hb, n)
        for k in range(KC):
            P.op("dve", _I.scalar_tensor_tensor(out=abf[:, k, t.c0:t.c0 + n], in0=hsrc(k), scalar=gcolap[:, k:k + 1],
                                                              in1=rstd[:, ri, 0:n], op0=ALU.mult, op1=ALU.mult),
                 reads=hb + [B("rstd", ri), B("gam")], writes=t.abufs)

    bank_ctr = [0]

    def next_bank(lo, hi):
        b = lo + bank_ctr[0] % (hi - lo)
        bank_ctr[0] += 1
        return b

    def h_add(t, m, bank, scalar_ap=None, mult=False):
        dst = t.hap(m, m + 1)[:, 0, :]
        if scalar_ap is None:
            P.op("dve", _I.tensor_tensor(out=dst, in0=ps[bank][:, 0:t.n], in1=dst, op=ALU.add),
                 reads=[psb[bank]] + t.hbufs, writes=t.hbufs)
        else:
            P.op("dve", _I.scalar_tensor_tensor(out=dst, in0=ps[bank][:, 0:t.n], scalar=scalar_ap, in1=dst,
                                                         op0=(ALU.mult if mult else ALU.add), op1=ALU.add),
                 reads=[psb[bank]] + t.hbufs, writes=t.hbufs)

    def proj(s, w_off, tiles, src3, src_bufs_fn, consume, nk=KC, wstride=128):
        for t in tiles:
            bank = next_bank(0, 6)
            for k in range(nk):
                P.op("pe", _I.matmul(
                    ps[bank][:, 0:t.n], lhsT=ring[:, s, w_off + k * wstride:w_off + k * wstride + 128],
                    rhs=src3[:, k, t.c0:t.c0 + t.n], start=(k == 0), stop=(k == nk - 1)),
                    reads=[B("ring", s)] + src_bufs_fn(t, k), writes=[psb[bank]], inc=(k == nk - 1))
            consume(t, bank)

    FGRP = ((0, 8), (8, 7), (15, 7))

    def ffn_pass(l, tiles):
        P.alias(ET_ALL, ABF_ALL)
        for t in tiles:
            norm_to_abf(t, gcol["ffn"](l))
        for g, (j0, nj) in enumerate(FGRP):
            for jj in range(nj):
                s = load_unit(gu_d[l, j0 + jj], 2 * KC * 128)
                for t in tiles:
                    bg = next_bank(0, 6)
                    bu = next_bank(0, 6)
                    for which, bank in ((0, bg), (1, bu)):
                        for k in range(KC):
                            off = which * KC * 128 + k * 128
                            P.op("pe", _I.matmul(
                                ps[bank][:, 0:t.n], lhsT=ring[:, s, off:off + 128], rhs=abf[:, k, t.c0:t.c0 + t.n],
                                start=(k == 0), stop=(k == KC - 1)),
                                reads=[B("ring", s)] + t.abufs, writes=[psb[bank]], inc=(k == KC - 1))
                    si = next_bank(0, 2)
                    P.op("act", _I.activation(out=sgt[:, si, 0:t.n], in_=ps[bg][:, 0:t.n], func=AF.Silu),
                         reads=[psb[bg]], writes=[B("sgt", si)])
                    P.op("dve", _I.tensor_tensor(
                        out=gbf[:, jj, t.c0:t.c0 + t.n], in0=ps[bu][:, 0:t.n], in1=sgt[:, si, 0:t.n], op=ALU.mult),
                        reads=[psb[bu], B("sgt", si)], writes=t.gb(jj))
            for m in range(KC):
                s = load_unit(dn_d[l, g, m][:, 0:nj * 128], nj * 128)
                for t in tiles:
                    bank = next_bank(0, 6)
                    for jj in range(nj):
                        P.op("pe", _I.matmul(
                            ps[bank][:, 0:t.n], lhsT=ring[:, s, jj * 128:(jj + 1) * 128], rhs=gbf[:, jj, t.c0:t.c0 + t.n],
                            start=(jj == 0), stop=(jj == nj - 1)),
                            reads=[B("ring", s)] + t.gb(jj), writes=[psb[bank]], inc=(jj == nj - 1))
                    h_add(t, m, bank)

    att_ctr = [0]
    ones_ok = {}

    def vslot(sl):
        return 0 if sl == "L" else sl + 1

    def qbufs(kv, sl):
        return [B("gbf", 2 * kv, sl), B("gbf", 2 * kv + 1, sl)]

    def bcast_q(ap2, nq):
        return bass.AP(ap2.tensor, ap2.offset, [list(ap2.ap[0]), list(ap2.ap[1]), [0, nq]])

    def attention_block(qsl, qn, kv, chunks, kind):
        ai = att_ctr[0] % 2
        att_ctr[0] += 1
        qc0 = scol(qsl)
        W2 = 2 * qn
        W4 = 4 * qn
        sbanks = []
        for ci, ch in enumerate(chunks):
            nk = ch["nk"]
            kc0 = scol(ch["kt_sl"])
            bank = (ci if kind == "r" else 2) + 3 * ai
            sbanks.append(bank)
            for hi in range(2):
                P.op("pe", _I.matmul(
                    ps[bank][0:nk, hi * W2:(hi + 1) * W2].rearrange("p (a b) -> p a b", a=2),
                    lhsT=kt[:, 2 * kv + hi, kc0:kc0 + nk], rhs=qt[:, 2 * kv:2 * kv + 2, qc0:qc0 + qn], start=True, stop=True),
                    reads=[B("kt", ch["kt_sl"])] + qbufs(kv, qsl), writes=[psb[bank]], inc=(hi == 1))
        pv = []
        for ci, ch in enumerate(chunks):
            nk = ch["nk"]
            bank = sbanks[ci]
            if ch["band"]:
                e_ap = Et[:, ai, ci, 0:W4]; e_buf = B("Et", ai, ci)
                p_ap = Pt[:, ai, ci, 0:W4]; p_buf = B("Pt", ai, ci)
                full_p = p_ap
            elif kind == "r":
                e_ap = Em[0:NMETA, ai, 0:W4]; e_buf = B("Em", ai)
                p_ap = Pm[0:NMETA, kv, 0:W4]; p_buf = B("Pm", kv)
                full_p = Pm[0:33, kv, 0:W4]
            else:
                e_ap = Em[0:NMETA, ai, 0:W4]; e_buf = B("Em", ai)
                p_ap = Pmm[0:NMETA, kv, 0:W4]; p_buf = B("Pmm", kv)
                full_p = Pmm[0:33, kv, 0:W4]
            P.op("act", _I.activation(out=e_ap, in_=ps[bank][0:nk, 0:W4], func=AF.Exp, scale=0.125),
                 reads=[psb[bank]], writes=[e_buf])
            if ch.get("ebq"):
                in0 = e_ap.rearrange("p (s q) -> p s q", s=4)
                outp = p_ap.rearrange("p (s q) -> p s q", s=4)
                P.op("dve", _I.tensor_tensor(out=outp, in0=in0, in1=ch["eb_ap"], op=ALU.mult),
                     reads=[e_buf, B(ch["eb_key"])], writes=[p_buf])
            else:
                P.op("dve", _I.tensor_tensor(out=p_ap, in0=e_ap, in1=ch["eb_ap"], op=ALU.mult),
                     reads=[e_buf, B(ch["eb_key"])], writes=[p_buf])
            pv.append((ch["vx_ap"], ch["vx_buf"], full_p, p_buf))
        obank = 6 + ai
        for ci, (vx_ap, vx_buf, full_p, p_buf) in enumerate(pv):
            P.op("pe", _I.matmul(
                ps[obank][:, 0:W4], lhsT=vx_ap, rhs=full_p, start=(ci == 0), stop=(ci == len(pv) - 1)),
                reads=[vx_buf, p_buf], writes=[psb[obank]], inc=(ci == len(pv) - 1))
        P.op("dve", _I.reciprocal(out=rden[64:128, ai, 0:W4], in_=ps[obank][64:128, 0:W4]),
             reads=[psb[obank]], writes=[B("sgt", ai)])
        for hi in range(2):
            o_out = qt[hi * 64:(hi + 1) * 64, 2 * kv:2 * kv + 2, qc0:qc0 + qn]
            o_in = ps[obank][0:64, hi * W2:(hi + 1) * W2].rearrange("p (a b) -> p a b", a=2)
            r_in = rden[64:128, ai, hi * W2:(hi + 1) * W2].rearrange("p (a b) -> p a b", a=2)
            P.op("dve", _I.tensor_tensor(out=o_out, in0=o_in, in1=r_in, op=ALU.mult),
                 reads=[psb[obank], B("sgt", ai)], writes=qbufs(kv, qsl))

    def attn_pass(l, j, pi, blocks):
        first = (pi == 0)
        left_block = blocks[0] - 1
        tiles = make_tiles(blocks, with_meta=first, left_block=(left_block if first else None))
        rtiles = [t for t in tiles if t.kind == "r"]
        qtiles = [t for t in tiles if t.kind != "L"]
        P.alias(ET_ALL, ABF_ALL)
        for t in tiles:
            norm_to_abf(t, gcol["mix"](l))
        if first:
            scr_claim([B("ebmf")])
            P.op("sp", _I.dma_start(out=ebmf, in_=bmf_d), writes=[B("ebmf")], dma=("ebmf", l))
            P.op("act", _I.activation(out=ebmf, in_=ebmf, func=AF.Exp), reads=[B("ebmf")], writes=[B("ebmf")])
            srow = sinkst[32:33, j * 16:(j + 1) * 16]
            P.op("act", _I.activation(out=Pm[32:33, :, :].rearrange("p a (s q) -> p (a s) q", s=4),
                                               in_=bcast_q(srow, 128), func=AF.Exp),
                 reads=[B("sinkst", j)], writes=[B("Pm", kv) for kv in range(4)])
            P.op("act", _I.activation(out=Pmm[32:33, :, :].rearrange("p a (s q) -> p (a s) q", s=4),
                                               in_=bcast_q(srow, NMETA), func=AF.Exp),
                 reads=[B("sinkst", j)], writes=[B("Pmm", kv) for kv in range(4)])
        abf_src = lambda t, k: t.abufs
        for c in range(8):
            s = load_unit(aq_d[j, c], KC * 128)

            def cons(t, bank, c=c):
                P.op("act", _I.activation(out=qt[:, c, t.c0:t.c0 + t.n], in_=ps[bank][:, 0:t.n], func=AF.Identity,
                                                   bias=attb[:, j, c:c + 1], scale=1.0),
                     reads=[psb[bank], B("attb", j)], writes=t.gb(c))
            proj(s, 0, qtiles, abf, abf_src, cons)
        for c in range(8):
            s = load_unit(ak_d[j, c], KC * 128)

            def cons(t, bank, c=c):
                P.op("act", _I.activation(out=kt[:, c, t.c0:t.c0 + t.n], in_=ps[bank][:, 0:t.n], func=AF.Identity,
                                                   bias=attb[:, j, 8 + c:9 + c], scale=1.0),
                     reads=[psb[bank], B("attb", j)], writes=t.ktb)
            proj(s, 0, tiles, abf, abf_src, cons)
        s = load_unit(av_d[j], KC * 256)
        for t in tiles:
            for sbi, sl in enumerate(t.slots):
                if t.kind == "m":
                    rows = NMETA; dst = vxm3[0:NMETA, :, 0:64]; wb = [B("vx", "m")]; gbk = None
                else:
                    rows = 128
                    gbk = t.blocks[sbi]
                    dst = vx4[:, vslot(sl), :, 0:64]
                    wb = [B("vx", sl)]
                    if not ones_ok.get(vslot(sl), True):
                        P.op("dve", _I.memset(vx4[:, vslot(sl), :, 64:128], 1.0), writes=wb)
                        ones_ok[vslot(sl)] = True
                cc0 = scol(sl)
                bank = next_bank(0, 6)
                for k in range(KC):
                    P.op("pe", _I.matmul(
                        ps[bank][0:rows, 0:256], lhsT=abf[:, k, cc0:cc0 + rows], rhs=ring[:, s, k * 256:(k + 1) * 256],
                        start=(k == 0), stop=(k == KC - 1)),
                        reads=[B("ring", s), B("abf", sl)], writes=[psb[bank]], inc=(k == KC - 1))
                P.op("dve", _I.tensor_tensor(
                    out=dst, in0=ps[bank][0:rows, 0:256].rearrange("p (a b) -> p a b", a=4),
                    in1=attbv[0:rows, j, :].rearrange("p (a b) -> p a b", a=4), op=ALU.add),
                    reads=[psb[bank], B("attbv", j)], writes=wb)
                if gbk == NHALO - 1:
                    vs = vslot(sl)
                    P.op("dve", _I.tensor_scalar_mul(out=vx[:, vs, :], in0=vx[:, vs, :], scalar1=flags[:, 1:2]),
                         reads=wb + [B("flags")], writes=wb)
                    ones_ok[vs] = False
        P.alias(ABF_ALL, ET_ALL)

        def meta_chunk(kv, first_blk):
            d = dict(kt_sl="m", nk=NMETA, vx_ap=vxm[0:33, kv * 128:(kv + 1) * 128], vx_buf=B("vx", "m"), band=False)
            if first_blk == "mm":
                d.update(eb_ap=ebmm[:, kv * 64:(kv + 1) * 64], eb_key="ebmm")
            elif first_blk:
                d.update(eb_ap=ebmf[:, kv * 512:(kv + 1) * 512], eb_key="ebmf")
            else:
                d.update(eb_ap=bcast_q(ebmr[:, kv * 4:(kv + 1) * 4], 128), eb_key="ebmr", ebq=True)
            return d
        if first:
            for kv in range(4):
                attention_block("m", NMETA, kv, [meta_chunk(kv, "mm")], "m")
        for t in rtiles:
            for sbi, sl in enumerate(t.slots):
                gbk = t.blocks[sbi]
                psl = "L" if sl == 0 else sl - 1
                for kv in range(4):
                    chunks = [
                        dict(kt_sl=psl, nk=128, vx_ap=vx[:, vslot(psl), kv * 128:(kv + 1) * 128], vx_buf=B("vx", psl),
                             eb_ap=ebp[:, kv * 512:(kv + 1) * 512], eb_key="ebp", band=True),
                        dict(kt_sl=sl, nk=128, vx_ap=vx[:, vslot(sl), kv * 128:(kv + 1) * 128], vx_buf=B("vx", sl),
                             eb_ap=ebc[:, kv * 512:(kv + 1) * 512], eb_key="ebc", band=True),
                        meta_chunk(kv, gbk == NHALO),
                    ]
                    attention_block(sl, 128, kv, chunks, "r")
        for m in range(8):
            s = load_unit(ao_d[j, m], KC * 128)

            def cons(t, bank, m=m):
                h_add(t, m, bank, scalar_ap=attb[:, j, 16 + m:17 + m])
            proj(s, 0, qtiles, qt, lambda t, k: t.gb(k), cons)
        return tiles

    def attn_carry(blocks):
        lsl = len(blocks) - 1
        lc0 = scol(lsl)
        P.op("dve", _I.tensor_copy(out=kt[:, :, NMETA:NMETA + 128], in_=kt[:, :, lc0:lc0 + 128]),
             reads=[B("kt", lsl)], writes=[B("kt", "L")])
        P.op("dve", _I.tensor_copy(out=vx[:, 0, :], in_=vx[:, vslot(lsl), :]),
             reads=[B("vx", lsl)], writes=[B("vx", "L")])
        ones_ok[0] = ones_ok.get(vslot(lsl), True)

    def conv_pass(l, pi, blocks):
        first = (pi == 0)
        tiles = make_tiles(blocks, with_meta=first)
        P.alias(ET_ALL, ABF_ALL)
        for t in tiles:
            norm_to_abf(t, gcol["mix"](l))
        if first:
            scr_claim([B("zt", 0), B("zt", 1), B("cacc", 0), B("cacc", 1)])
        ybf = gbf
        for c in range(8):
            slots3 = [load_unit(ci_d[c, part], KC * 128) for part in range(3)]
            for t in tiles:
                banks = [next_bank(0, 6) for _ in range(3)]
                for part in range(3):
                    for k in range(KC):
                        P.op("pe", _I.matmul(
                            ps[banks[part]][:, 0:t.n], lhsT=ring[:, slots3[part], k * 128:(k + 1) * 128], rhs=abf[:, k, t.c0:t.c0 + t.n],
                            start=(k == 0), stop=(k == KC - 1)),
                            reads=[B("ring", slots3[part])] + t.abufs, writes=[psb[banks[part]]], inc=(k == KC - 1))
                bB, bC, bU = banks
                zi = next_bank(0, 2)
                n = t.n
                P.op("act", _I.copy(out=sgt[:, zi, 0:n], in_=ps[bU][:, 0:n]),
                     reads=[psb[bU]], writes=[B("sgt", zi)])
                if t.kind == "m":
                    P.op("dve", _I.memset(zt[:, zi, 0:2], 0.0), writes=[B("zt", zi)])
                else:
                    P.op("dve", _I.tensor_copy(out=zt[:, zi, 0:2], in_=zcar[:, c, :]),
                         reads=[B("zcar")], writes=[B("zt", zi)])
                P.op("dve", _I.tensor_tensor(out=zt[:, zi, 2:2 + n], in0=ps[bC][:, 0:n], in1=sgt[:, zi, 0:n], op=ALU.mult),
                     reads=[psb[bC], B("sgt", zi)], writes=[B("zt", zi)])
                if t.kind != "m":
                    P.op("dve", _I.tensor_copy(out=zcar[:, c, :], in_=zt[:, zi, n:n + 2]),
                         reads=[B("zt", zi)], writes=[B("zcar")])
                P.op("dve", _I.tensor_scalar_mul(out=cacc[:, zi, 0:n], in0=zt[:, zi, 0:n], scalar1=cw[:, c:c + 1]),
                     reads=[B("zt", zi), B("cw")], writes=[B("cacc", zi)])
                for tap in (1, 2):
                    P.op("dve", _I.scalar_tensor_tensor(
                        out=cacc[:, zi, 0:n], in0=zt[:, zi, tap:tap + n], scalar=cw[:, tap * KC + c:tap * KC + c + 1],
                        in1=cacc[:, zi, 0:n], op0=ALU.mult, op1=ALU.add),
                        reads=[B("zt", zi), B("cw"), B("cacc", zi)], writes=[B("cacc", zi)])
                P.op("dve", _I.tensor_tensor(out=ybf[:, c, t.c0:t.c0 + n], in0=ps[bB][:, 0:n],
                                                                                  in1=cacc[:, zi, 0:n], op=ALU.mult),
                     reads=[psb[bB], B("cacc", zi)], writes=t.gb(c))
        for m in range(8):
            s = load_unit(co_d[m], KC * 128)

            def cons(t, bank, m=m):
                h_add(t, m, bank)
            proj(s, 0, tiles, ybf, lambda t, k: t.gb(k), cons)
        return tiles

    def pool_pass(l, pi, blocks):
        first = (pi == 0)
        tiles = make_tiles(blocks, with_meta=first)
        mixbf = gbf
        if first:
            scr_claim([B("afk", 0), B("afk", 1)] + [B("pls", i) for i in range(3)])
        s = load_unit(pw_d, 4 * 2 * 256)
        seen_real = False
        for t in tiles:
            n = t.n
            if t.kind == "m":
                hsrc, hb, nn = tile_hsrc(t)
                mode = "zero"; off = NMETA
            elif first and not seen_real:
                hc0 = t.blocks[0] * 128 - NMETA
                nn = n + NMETA
                hsrc = lambda k, hc0=hc0, nn=nn: h[:, k, hc0:hc0 + nn]
                hb = t.hbufs + [B("h", t.blocks[0] - 1)]
                mode = "own"; off = 0
            else:
                hsrc, hb, nn = tile_hsrc(t)
                mode = "carry"; off = NMETA
            ri = rms_stats(hsrc, hb, nn)
            W = NMETA + n
            for k in range(KC):
                ki = k % 2
                if mode == "zero":
                    P.op("dve", _I.memset(afk[:, ki, 0:NMETA], 0.0), writes=[B("afk", ki)])
                elif mode == "carry":
                    P.op("dve", _I.tensor_copy(out=afk[:, ki, 0:NMETA], in_=acar[:, k, :]),
                         reads=[B("acar")], writes=[B("afk", ki)])
                P.op("dve", _I.scalar_tensor_tensor(
                    out=afk[:, ki, off:off + nn], in0=hsrc(k), scalar=gcol["mix"](l)[:, k:k + 1], in1=rstd[:, ri, 0:nn],
                    op0=ALU.mult, op1=ALU.mult),
                    reads=hb + [B("rstd", ri), B("gam")], writes=[B("afk", ki)])
                if t.kind != "m":
                    P.op("dve", _I.tensor_copy(out=acar[:, k, :], in_=afk[:, ki, n:n + NMETA]),
                         reads=[B("afk", ki)], writes=[B("acar")])
                w = POOL_W[k]
                step = 1
                si = 0
                while step < w:
                    dsti = si % 3
                    lo = 2 * step - 1
                    if step == 1:
                        in0 = afk[:, ki, lo:W]; in1 = afk[:, ki, lo - step:W - step]; rb = [B("afk", ki)]
                    else:
                        prev = (si - 1) % 3
                        in0 = pls[:, prev, lo:W]; in1 = pls[:, prev, lo - step:W - step]; rb = [B("pls", prev)]
                    P.op("dve", _I.tensor_tensor(out=pls[:, dsti, lo:W], in0=in0, in1=in1, op=ALU.add),
                         reads=rb, writes=[B("pls", dsti)])
                    step *= 2
                    si += 1
                last = (si - 1) % 3
                if t.kind == "m":
                    P.op("dve", _I.tensor_tensor(out=pls[:, last, NMETA:W], in0=pls[:, last, NMETA:W],
                                                                           in1=pic[:, k * NMETA:(k + 1) * NMETA], op=ALU.mult),
                         reads=[B("pls", last), B("pic")], writes=[B("pls", last)])
                    P.op("dve", _I.tensor_tensor(out=mixbf[:, k, t.c0:t.c0 + n], in0=pls[:, last, NMETA:W],
                                                                                       in1=afk[:, ki, NMETA:W], op=ALU.subtract),
                         reads=[B("pls", last), B("afk", ki)], writes=t.gb(k))
                else:
                    P.op("dve", _I.scalar_tensor_tensor(
                        out=mixbf[:, k, t.c0:t.c0 + n], in0=pls[:, last, NMETA:W], scalar=1.0 / w, in1=afk[:, ki, NMETA:W],
                        op0=ALU.mult, op1=ALU.subtract),
                        reads=[B("pls", last), B("afk", ki)], writes=t.gb(k))
            if t.kind != "m":
                seen_real = True
            for g in range(4):
                for dc in range(2):
                    m = 2 * g + dc
                    bank = next_bank(0, 6)
                    for cc in range(2):
                        off2 = (g * 2 + cc) * 256 + dc * 128
                        P.op("pe", _I.matmul(
                            ps[bank][:, 0:t.n], lhsT=ring[:, s, off2:off2 + 128], rhs=mixbf[:, 2 * g + cc, t.c0:t.c0 + t.n],
                            start=(cc == 0), stop=(cc == 1)),
                            reads=[B("ring", s)] + t.gb(2 * g + cc), writes=[psb[bank]], inc=(cc == 1))
                    h_add(t, m, bank, scalar_ap=psc[:, m:m + 1], mult=True)
        return tiles

    def blend():
        c0 = NHALO * 128 - NMETA
        P.op("dve", _I.tensor_scalar_mul(out=tmpb[:], in0=hm[:], scalar1=flags[:, 0:1]),
             reads=[B("hm"), B("flags")], writes=[B("tmpb")])
        P.op("dve", _I.scalar_tensor_tensor(out=h[:, :, c0:c0 + NMETA], in0=h[:, :, c0:c0 + NMETA], scalar=flags[:, 1:2],
                                                     in1=tmpb[:], op0=ALU.mult, op1=ALU.add),
             reads=[B("tmpb"), B("h", NHALO - 1), B("flags")], writes=[B("h", NHALO - 1)])

    q0s = [1, 1, 2, 3]
    for l in range(n_layers):
        kind, j = l % 3, l // 3
        q0 = min(q0s[l], NB - 1)
        if kind in (1, 2):
            blend()
        passes = _passes(q0, NB)
        for pi, blocks in enumerate(passes):
            if kind == 0:
                tiles = attn_pass(l, j, pi, blocks)
                if pi + 1 < len(passes):
                    attn_carry(blocks)
            elif kind == 1:
                tiles = conv_pass(l, pi, blocks)
            else:
                tiles = pool_pass(l, pi, blocks)
            ffn_pass(l, [t for t in tiles if t.kind != "L"])

    out_ctr = [0]
    scr_claim([B("stage", 0), B("stage", 1), B("yf")])

    def emit_rows(src_fn, src_bufs, nrows, row0):
        oi = out_ctr[0] % 2
        out_ctr[0] += 1
        for half in range(2):
            bank = 2 * oi + half
            for kk in range(4):
                k = half * 4 + kk
                P.op("pe", _I.transpose(
                    out=ps[bank][0:nrows, kk * 128:(kk + 1) * 128], in_=src_fn(k), identity=ident[:]),
                    reads=src_bufs + [B("ident")], writes=[psb[bank]], inc=(kk == 3))
            if half == 0:
                P.op("act", _I.copy(out=stage[0:nrows, oi, 0:512], in_=ps[bank][0:nrows, :]),
                     reads=[psb[bank]], writes=[B("stage", oi)])
            else:
                P.op("dve", _I.tensor_copy(out=stage[0:nrows, oi, 512:1024], in_=ps[bank][0:nrows, :]),
                     reads=[psb[bank]], writes=[B("stage", oi)])
        P.op("sp", _I.dma_start(out=out_d[row0:row0 + nrows, :], in_=stage[0:nrows, oi, :]),
             reads=[B("stage", oi)], dma=("outq", oi))

    if debug_h:
        for bi in range(n_own):
            blk = NHALO + bi
            emit_rows(lambda k, blk=blk: h[:, k, blk * 128:(blk + 1) * 128], [B("h", blk)], 128, bi * 128)
        emit_rows(lambda k: hm[:, k, :], [B("hm")], NMETA, n_own * 128)
    else:
        for bi in range(n_own):
            blk = NHALO + bi
            hsrc = lambda k, blk=blk: h[:, k, blk * 128:(blk + 1) * 128]
            hb = [B("h", blk)]
            ri = rms_stats(hsrc, hb, 128)
            for k in range(KC):
                P.op("dve", _I.scalar_tensor_tensor(
                    out=yf[:, k, :], in0=hsrc(k), scalar=gcol["fin"](0)[:, k:k + 1], in1=rstd[:, ri, 0:128],
                    op0=ALU.mult, op1=ALU.mult),
                    reads=hb + [B("rstd", ri), B("gam")], writes=[B("yf")])
            emit_rows(lambda k: yf[:, k, :], [B("yf")], 128, bi * 128)

    P.emit(nc, final_waits=tuple(k for k in (("outq", 0), ("outq", 1)) if k in P.cnt))
    return nc


def _prep_shared(inp):
    f = lambda a: np.ascontiguousarray(np.asarray(a, dtype=np.float32))
    sh = {}
    sh["meta"] = f(inp["meta_tokens"])
    sh["ident"] = np.eye(128, dtype=np.float32)
    gam = np.concatenate([_fm(f(inp["norm_mix"])).reshape(128, 4 * KC), _fm(f(inp["norm_ffn"])).reshape(128, 4 * KC),
                          _fm(f(inp["norm_final"])).reshape(128, KC)], axis=1)
    sh["gammas"] = np.ascontiguousarray(gam)
    wg, wu, wd = f(inp["ffn_w_gate"]), f(inp["ffn_w_up"]), f(inp["ffn_w_down"])
    gu = np.empty((4, FC, 128, 2, KC, 128), np.float32)
    dn = np.zeros((4, 3, KC, 128, 8, 128), np.float32)
    for l in range(4):
        gu[l, :, :, 0] = _arr(wg[l])
        gu[l, :, :, 1] = _arr(wu[l])
        a = _arr(wd[l])
        dn[l, 0, :, :, 0:8] = a[:, :, 0:8, :]
        dn[l, 1, :, :, 0:7] = a[:, :, 8:15, :]
        dn[l, 2, :, :, 0:7] = a[:, :, 15:22, :]
    sh["ffn_gu"] = gu.reshape(4, FC, 128, 2 * KC * 128)
    sh["ffn_dn"] = dn.reshape(4, 3, KC, 128, 8 * 128)
    wqkv, bqkv = f(inp["attn_w_qkv"]), f(inp["attn_b_qkv"])
    wo, bo = f(inp["attn_w_o"]), f(inp["attn_b_o"])
    aq = np.empty((2, 8, 128, KC, 128), np.float32)
    ak = np.zeros((2, 8, 128, KC, 128), np.float32)
    av = np.empty((2, 128, KC, 256), np.float32)
    ao = np.empty((2, 8, 128, KC, 128), np.float32)
    ab = np.zeros((2, 128, 24), np.float32)
    abv = np.empty((2, 128, 256), np.float32)
    for j in range(2):
        aq[j] = _arr(wqkv[j][:, 0:1024])
        wk = wqkv[j][:, 1024:1280]
        bk = bqkv[j][1024:1280]
        for kv in range(4):
            wkk = wk[:, kv * 64:(kv + 1) * 64].reshape(KC, 128, 64).transpose(1, 0, 2)
            ak[j, 2 * kv, :, :, 0:64] = wkk
            ak[j, 2 * kv + 1, :, :, 64:128] = wkk
            ab[j, 0:64, 8 + 2 * kv] = bk[kv * 64:(kv + 1) * 64]
            ab[j, 64:128, 8 + 2 * kv + 1] = bk[kv * 64:(kv + 1) * 64]
        av[j] = wqkv[j][:, 1280:1536].reshape(KC, 128, 256).transpose(1, 0, 2)
        ao[j] = _arr(wo[j])
        ab[j, :, 0:8] = _fm(bqkv[j][0:1024])
        ab[j, :, 16:24] = _fm(bo[j])
        abv[j] = np.broadcast_to(bqkv[j][1280:1536], (128, 256))
    sh["attn_q"] = aq.reshape(2, 8, 128, KC * 128)
    sh["attn_k"] = ak.reshape(2, 8, 128, KC * 128)
    sh["attn_v"] = av.reshape(2, 128, KC * 256)
    sh["attn_o"] = ao.reshape(2, 8, 128, KC * 128)
    sh["attn_b"] = ab
    sh["attn_bv"] = abv
    sinks = f(inp["attn_sinks"])
    sk = np.empty((2, 4, 4), np.float32)
    for j in range(2):
        for kv in range(4):
            for s_, g in enumerate(SLOT_ORDER):
                sk[j, kv, s_] = sinks[j, 4 * kv + g]
    sh["attn_sink"] = sk.reshape(2, 16)
    tab = f(inp["rel_bias_table"])
    bt = _bucket_table()
    kk = np.arange(128)[:, None]
    qq = np.arange(128)[None, :]
    d_prev = 128 + qq - kk
    d_cur = qq - kk
    m_prev = ((d_prev >= 0) & (d_prev < 128)).astype(np.float32)
    m_cur = ((d_cur >= 0) & (d_cur < 128)).astype(np.float32)
    bp = np.empty((128, 4, 4, 128), np.float32)
    bc = np.empty((128, 4, 4, 128), np.float32)
    for kv in range(4):
        for s_, g in enumerate(SLOT_ORDER):
            hh = 4 * kv + g
            bp[:, kv, s_, :] = tab[bt[np.clip(d_prev, 0, 511)], hh]
            bc[:, kv, s_, :] = tab[bt[np.clip(d_cur, 0, 511)], hh]
    sh["bias_prev"] = bp.reshape(128, 2048)
    sh["bias_cur"] = bc.reshape(128, 2048)
    sh["mask_prev"] = np.ascontiguousarray(np.broadcast_to(m_prev[:, None, None, :], (128, 4, 4, 128))).reshape(128, 2048)
    sh["mask_cur"] = np.ascontiguousarray(np.broadcast_to(m_cur[:, None, None, :], (128, 4, 4, 128))).reshape(128, 2048)
    mm_ = np.arange(NMETA)[:, None]
    d_first = NMETA + qq - mm_
    bmf = np.empty((NMETA, 4, 4, 128), np.float32)
    bmr = np.empty((NMETA, 4, 4), np.float32)
    mq = np.arange(NMETA)[None, :]
    d_mm = mq - mm_
    bmm = np.empty((NMETA, 4, 4, NMETA), np.float32)
    for kv in range(4):
        for s_, g in enumerate(SLOT_ORDER):
            hh = 4 * kv + g
            bmf[:, kv, s_, :] = tab[bt[d_first], hh]
            bmr[:, kv, s_] = tab[31, hh]
            bmm[:, kv, s_, :] = tab[bt[np.clip(d_mm, 0, 511)], hh]
    sh["_bmf"] = bmf.reshape(NMETA, 2048)
    sh["bias_meta_rest"] = bmr.reshape(NMETA, 16)
    sh["_bmr_full"] = np.ascontiguousarray(np.broadcast_to(bmr[:, :, :, None], (NMETA, 4, 4, 128))).reshape(NMETA, 2048)
    sh["bias_mm"] = bmm.reshape(NMETA, 256)
    sh["mask_mm"] = np.ascontiguousarray(np.broadcast_to((d_mm >= 0).astype(np.float32)[:, None, None, :], (NMETA, 4, 4, NMETA))).reshape(NMETA, 256)
    cin = f(inp["conv_w_in"])[0]
    ci = np.empty((8, 3, 128, KC, 128), np.float32)
    for part in range(3):
        ci[:, part] = _arr(cin[:, part * 1024:(part + 1) * 1024])
    sh["conv_in"] = ci.reshape(8, 3, 128, KC * 128)
    sh["conv_out"] = _arr(f(inp["conv_w_out"])[0]).reshape(8, 128, KC * 128)
    sh["conv_w"] = _fm(f(inp["conv_w"])[0]).reshape(128, 3 * KC)
    pw = f(inp["pool_w"])[0]
    sh["pool_w"] = np.ascontiguousarray(pw.reshape(4, 2, 128, 256).transpose(2, 0, 1, 3)).reshape(128, 2048)
    sh["pool_scale"] = _fm(f(inp["pool_scale"])[0]).reshape(128, KC)
    invc = np.empty((128, KC, NMETA), np.float32)
    tt = np.arange(NMETA)
    for k in range(KC):
        invc[:, k, :] = (1.0 / np.minimum(POOL_W[k], tt + 1)).astype(np.float32)[None, :]
    sh["pool_invc"] = invc.reshape(128, KC * NMETA)
    return sh


def _core_inputs(x, sh, c, n_own):
    b, ch = divmod(c, SEQ // CHUNK)
    s0 = ch * CHUNK
    T = (NHALO + n_own) * 128
    xc = np.zeros((T, D), np.float32)
    lo = s0 - NHALO * 128
    src_lo = max(lo, 0)
    hi = s0 + n_own * 128
    xc[src_lo - lo:, :] = x[b, src_lo:hi, :]
    m = {k: v for k, v in sh.items() if not k.startswith("_")}
    m["x"] = xc
    fl = np.zeros((128, 2), np.float32)
    fl[:, 0] = 1.0 if ch == 0 else 0.0
    fl[:, 1] = 0.0 if ch == 0 else 1.0
    m["flags"] = fl
    m["bias_meta_first"] = sh["_bmf"] if ch == 0 else sh["_bmr_full"]
    return m


_NC_CACHE = {}


def kernel(x, meta_tokens, rel_bias_table, norm_mix, norm_ffn, norm_final,
           attn_w_qkv, attn_b_qkv, attn_w_o, attn_b_o, attn_sinks,
           conv_w_in, conv_w, conv_w_out, pool_w, pool_scale,
           ffn_w_gate, ffn_w_up, ffn_w_down):
    inp = dict(meta_tokens=meta_tokens, rel_bias_table=rel_bias_table, norm_mix=norm_mix, norm_ffn=norm_ffn,
               norm_final=norm_final, attn_w_qkv=attn_w_qkv, attn_b_qkv=attn_b_qkv, attn_w_o=attn_w_o,
               attn_b_o=attn_b_o, attn_sinks=attn_sinks, conv_w_in=conv_w_in, conv_w=conv_w, conv_w_out=conv_w_out,
               pool_w=pool_w, pool_scale=pool_scale, ffn_w_gate=ffn_w_gate, ffn_w_up=ffn_w_up, ffn_w_down=ffn_w_down)
    x = np.ascontiguousarray(np.asarray(x, dtype=np.float32))
    sh = _prep_shared(inp)
    n_own = CHUNK // 128
    if "nc" not in _NC_CACHE:
        _NC_CACHE["nc"] = build(n_own=n_own)
    nc = _NC_CACHE["nc"]
    in_maps = [_core_inputs(x, sh, c, n_own) for c in range(N_CORES)]
    res = run_bass_kernel_spmd(nc, in_maps, core_ids=list(range(N_CORES)))
    out = np.empty((x.shape[0], SEQ, D), np.float32)
    for c in range(N_CORES):
        b, ch = divmod(c, SEQ // CHUNK)
        out[b, ch * CHUNK:(ch + 1) * CHUNK, :] = res.results[c]["out"]
    return out
```

```python
import os
import numpy as np
import concourse.bass as bass
import concourse.mybir as mybir
from concourse.bass_utils import run_bass_kernel_spmd

F32 = mybir.dt.float32
BF16 = mybir.dt.bfloat16
AF = mybir.ActivationFunctionType
ALU = mybir.AluOpType

D = 1024
KC = 8
DFF = 2816
FC = 22
NMETA = 16
NHALO = 3
EPS = 1e-6
N_CORES = 8
SEQ = 8192
CHUNK = 2048
POOL_W = (2, 2, 4, 4, 8, 8, 16, 16)
SLOT_ORDER = (0, 2, 1, 3)
RING_ELEMS = 2048
NSLOT = 6


class _Rec:
    def __getattr__(self, name):
        def f(*a, **kw):
            return (name, a, kw)
        return f


_I = _Rec()


class Buf:
    __slots__ = ("lw", "rd")

    def __init__(self):
        self.lw = None
        self.rd = []


class Prog:
    ENGS = ("pe", "act", "dve", "pool", "sp")

    def __init__(self):
        self.q = {e: [] for e in self.ENGS}
        self.cnt = {e: 0 for e in self.ENGS}
        self.seen = {e: {} for e in self.ENGS}
        self.bufs = {}

    def buf(self, *key):
        b = self.bufs.get(key)
        if b is None:
            b = self.bufs[key] = Buf()
        return b

    def op(self, eng, fn, reads=(), writes=(), inc=True, dma=None):
        raw = {}
        war = {}

        def need(d, tok):
            if tok is None:
                return
            k, v = tok
            if d.get(k, 0) < v:
                d[k] = v

        for b in reads:
            need(raw, b.lw)
        for b in writes:
            need(raw, b.lw)
            for r in b.rd:
                need(war, r)
        waits = []
        seen = self.seen[eng]
        for d, is_war in ((raw, False), (war, True)):
            for k, v in d.items():
                if k == eng and (eng == "pe" or is_war or dma is not None):
                    continue
                if seen.get(k, 0) < v:
                    seen[k] = v
                    waits.append((k, v))
        if dma is not None:
            self.cnt[dma] = self.cnt.get(dma, 0) + 16
            tok = (dma, self.cnt[dma])
            incinfo = (dma, 16)
        elif inc:
            self.cnt[eng] += 1
            tok = (eng, self.cnt[eng])
            incinfo = (eng, 1)
        else:
            tok = (eng, self.cnt[eng] + 1)
            incinfo = None
        for b in reads:
            b.rd.append(tok)
        for b in writes:
            b.lw = tok
            b.rd = []
        self.q[eng].append((waits, fn, incinfo))
        return tok

    def alias(self, old_bufs, new_bufs):
        toks = []
        for o in old_bufs:
            if o.lw is not None:
                toks.append(o.lw)
            toks.extend(o.rd)
        for n in new_bufs:
            n.rd.extend(toks)

    def emit(self, nc, final_waits):
        sems = {}
        for k in list(self.cnt.keys()):
            sems[k] = nc.alloc_semaphore("s_" + str(k).replace(" ", "").replace("'", "").replace("(", "").replace(")", "").replace(",", "_"))
        qmap = {"pe": "tensor", "act": "scalar", "dve": "vector", "pool": "gpsimd", "sp": "sync"}
        with nc.Block() as block:
            for eng in self.ENGS:
                lst = self.q[eng]
                fw = final_waits if eng == "sp" else ()

                def body(e, lst=lst, fw=fw):
                    for waits, fn, incinfo in lst:
                        for k, v in waits:
                            e.wait_ge(sems[k], v)
                        name, a_, kw_ = fn
                        ins = getattr(e, name)(*a_, **kw_)
                        if incinfo is not None:
                            ins.then_inc(sems[incinfo[0]], incinfo[1])
                    for k in fw:
                        e.wait_ge(sems[k], self.cnt[k])

                getattr(block, qmap[eng])(body)


def _arr(W):
    K, M = W.shape
    return np.ascontiguousarray(W.reshape(K // 128, 128, M // 128, 128).transpose(2, 1, 0, 3))


def _fm(v):
    lead = v.shape[:-1]
    r = v.reshape(lead + (KC, 128))
    r = np.moveaxis(r, -1, 0)
    return np.ascontiguousarray(r)


def _rel_bucket_np(d):
    d = np.maximum(d, 0)
    df = np.maximum(d, 1).astype(np.float32)
    large = 16 + (np.log(df / np.float32(16)) / np.float32(np.log(128 / 16)) * np.float32(16)).astype(np.int32)
    large = np.minimum(large, 31)
    return np.where(d < 16, d, large)


def _bucket_table():
    return _rel_bucket_np(np.arange(0, 512))


def _passes(q0, nb):
    blocks = list(range(q0, nb))
    n = len(blocks)
    npass = 3 if n >= 3 else 1
    base, rem = divmod(n, npass)
    out = []
    i = 0
    for p in range(npass):
        c = base + (1 if p < rem else 0)
        out.append(blocks[i:i + c])
        i += c
    return out


def _split_tiles(nblk):
    nt = (nblk + 3) // 4
    base, rem = divmod(nblk, nt)
    return [base + (1 if i < rem else 0) for i in range(nt)]


def build(n_own=16, n_layers=4, debug_h=False):
    NB = NHALO + n_own
    T = NB * 128
    nc = bass.Bass("TRN2", target_bir_lowering=False)
    P = Prog()
    B = P.buf

    def din(name, shape):
        return nc.dram_tensor(name, list(shape), F32, kind="ExternalInput").ap()

    x_d = din("x", [T, D])
    meta_d = din("meta", [NMETA, D])
    ident_d = din("ident", [128, 128])
    flag_d = din("flags", [128, 2])
    gam_d = din("gammas", [128, 9 * KC])
    gu_d = din("ffn_gu", [4, FC, 128, 2 * KC * 128])
    dn_d = din("ffn_dn", [4, 3, KC, 128, 8 * 128])
    aq_d = din("attn_q", [2, 8, 128, KC * 128])
    ak_d = din("attn_k", [2, 8, 128, KC * 128])
    av_d = din("attn_v", [2, 128, KC * 256])
    ao_d = din("attn_o", [2, 8, 128, KC * 128])
    ab_d = din("attn_b", [2, 128, 24])
    abv_d = din("attn_bv", [2, 128, 256])
    sink_d = din("attn_sink", [2, 16])
    ebp_d = din("bias_prev", [128, 4 * 512])
    ebc_d = din("bias_cur", [128, 4 * 512])
    mkp_d = din("mask_prev", [128, 4 * 512])
    mkc_d = din("mask_cur", [128, 4 * 512])
    bmf_d = din("bias_meta_first", [NMETA, 4 * 512])
    bmr_d = din("bias_meta_rest", [NMETA, 16])
    bmm_d = din("bias_mm", [NMETA, 4 * 64])
    mmm_d = din("mask_mm", [NMETA, 4 * 64])
    ci_d = din("conv_in", [8, 3, 128, KC * 128])
    co_d = din("conv_out", [8, 128, KC * 128])
    cw_d = din("conv_w", [128, 3 * KC])
    pw_d = din("pool_w", [128, 4 * 2 * 256])
    psc_d = din("pool_scale", [128, KC])
    pic_d = din("pool_invc", [128, KC * NMETA])
    out_rows = n_own * 128 + (128 if debug_h else 0)
    out_d = nc.dram_tensor("out", [out_rows, D], F32, kind="ExternalOutput").ap()

    def sb(name, shape, dt=F32):
        return nc.alloc_sbuf_tensor("sb_" + name, list(shape), dt)

    h = sb("h", [128, KC, T])
    hm = sb("hm", [128, KC, NMETA])
    maxpass = max(len(p) for q0 in (1, 2, 3) for p in _passes(min(q0, NB - 1), NB))
    NTP = max(NMETA + 128 + maxpass * 128, 768)
    abf_raw = sb("abf", [128, KC * NTP], BF16)
    abf = abf_raw[:].rearrange("p (k t) -> p k t", k=KC)
    Et = abf_raw[:, 0:4096].bitcast(F32).rearrange("p (a c w) -> p a c w", a=2, c=2)
    Em = abf_raw[:, 4096:6144].bitcast(F32).rearrange("p (a w) -> p a w", a=2)
    gbf = sb("gbf", [128, 8, NTP], BF16)
    qt = gbf
    kt = sb("kt", [128, 8, NTP], BF16)
    vx = sb("vx", [128, maxpass + 1, 4 * 128], BF16)
    vxm = sb("vxm", [64, 4 * 128], BF16)
    ring = sb("ring", [128, NSLOT, RING_ELEMS], BF16)
    ident = sb("ident", [128, 128])
    onesb = sb("onesb", [128, 128], BF16)
    flags = sb("flags", [128, 2])
    gam = sb("gam", [128, 9 * KC])
    epsb = sb("epsb", [128, 1])
    ebp = sb("ebp", [128, 4 * 512])
    ebc = sb("ebc", [128, 4 * 512])
    ebmr = sb("ebmr", [NMETA, 16])
    ebmm = sb("ebmm", [NMETA, 4 * 64])
    attb = sb("attb", [128, 2, 24])
    attbv = sb("attbv", [128, 2, 256])
    sinkst = sb("sinkst", [64, 2 * 16])
    cw = sb("cw", [128, 3 * KC])
    psc = sb("psc", [128, KC])
    pic = sb("pic", [128, KC * NMETA])
    sq = sb("sq", [128, 2, 512], BF16)
    rstd = sb("rstd", [128, 2, 512])
    sgt = sb("sgt", [128, 2, 512])
    rden = sgt
    SCR = 3072
    scr = sb("scr", [128, SCR])
    stage = scr[:, 1024:3072].rearrange("p (a w) -> p a w", a=2)
    mstage = scr[:, 0:2048]
    ebmf = scr[0:NMETA, 0:2048]
    zt = scr[:, 0:1028].rearrange("p (a w) -> p a w", a=2)
    cacc = scr[:, 1028:2052].rearrange("p (a w) -> p a w", a=2)
    afk = scr[:, 0:1056].rearrange("p (a w) -> p a w", a=2)
    pls = scr[:, 1056:2640].rearrange("p (a w) -> p a w", a=3)
    yfv = [abf_raw[:, i_ * 2048:(i_ + 1) * 2048].bitcast(F32).rearrange("p (k w) -> p k w", k=KC) for i_ in range(2)]
    Pt = sb("Pt", [128, 2, 2, 512], BF16)
    Pm = sb("Pm", [64, 4, 512], BF16)
    Pmm = sb("Pmm", [64, 4, 64], BF16)
    zcar = sb("zcar", [128, KC, 2])
    acar = sb("acar", [128, KC, NMETA])
    tmpb = sb("tmpb", [128, KC, NMETA])
    ps = [nc.alloc_psum_tensor("ps%d" % i, [128, 512], F32) for i in range(8)]
    psb = [B("ps", i) for i in range(8)]

    gcol = {"mix": lambda l: gam[:, l * KC:(l + 1) * KC], "ffn": lambda l: gam[:, (4 + l) * KC:(5 + l) * KC],
            "fin": lambda l: gam[:, 8 * KC:9 * KC]}

    scr_owner = [[]]

    def scr_claim(bufs):
        P.alias(scr_owner[0], bufs)
        scr_owner[0] = list(bufs)

    ABF_ALL = [B("abf", "m"), B("abf", "L")] + [B("abf", i) for i in range(maxpass)]
    ET_ALL = [B("Et", a_, c_) for a_ in range(2) for c_ in range(2)] + [B("Em", 0), B("Em", 1)]

    setup_bufs = []

    def sload(dst, src, *key):
        b = B(*key)
        setup_bufs.append(b)
        P.op("sp", _I.dma_start(out=dst, in_=src), writes=[b], dma="setup")
        return b

    sload(ident[:], ident_d, "ident")
    sload(flags[:], flag_d, "flags")
    sload(gam[:], gam_d, "gam")
    sload(ebp[:], ebp_d, "ebp")
    sload(ebc[:], ebc_d, "ebc")
    sload(ebmr[:], bmr_d, "ebmr")
    sload(ebmm[:], bmm_d, "ebmm")
    sload(tmpb[0:NMETA, :, :].rearrange("p a b -> p (a b)")[:, 0:128], mmm_d[:, 0:128], "tmpb")
    for l in range(2):
        sload(attb[:, l, :], ab_d[l], "attb", l)
        sload(attbv[:, l, :], abv_d[l], "attbv", l)
        sload(sinkst[32:33, l * 16:(l + 1) * 16], sink_d[l:l + 1, :], "sinkst", l)
    sload(cw[:], cw_d, "cw")
    sload(psc[:], psc_d, "psc")
    sload(pic[:], pic_d, "pic")
    tot = P.cnt["setup"]
    for b in setup_bufs:
        b.lw = ("setup", tot)

    P.op("dve", _I.memset(onesb[:], 1.0 / 1024.0), writes=[B("onesb")])
    P.op("dve", _I.memset(epsb[:], EPS), writes=[B("epsb")])
    for t_, key, md in ((ebp, "ebp", mkp_d), (ebc, "ebc", mkc_d)):
        scr_claim([B("mstage", key)])
        P.op("sp", _I.dma_start(out=mstage, in_=md), writes=[B("mstage", key)], dma=("mst", key))
        P.op("act", _I.activation(out=t_[:], in_=t_[:], func=AF.Exp), reads=[B(key)], writes=[B(key)])
        P.op("dve", _I.tensor_tensor(out=t_[:], in0=t_[:], in1=mstage, op=ALU.mult),
             reads=[B(key), B("mstage", key)], writes=[B(key)])
    for t_, key in ((ebmr, "ebmr"), (ebmm, "ebmm")):
        P.op("act", _I.activation(out=t_[:], in_=t_[:], func=AF.Exp), reads=[B(key)], writes=[B(key)])
    mm4 = ebmm[:].rearrange("p (a q) -> p a q", q=NMETA)
    msk = tmpb[0:NMETA, :, :].rearrange("p a b -> p (a b)")[:, 0:NMETA]
    msk_b = bass.AP(msk.tensor, msk.offset, [list(msk.ap[0]), [0, 16], [1, NMETA]])
    P.op("dve", _I.tensor_tensor(out=mm4, in0=mm4, in1=msk_b, op=ALU.mult),
         reads=[B("ebmm"), B("tmpb")], writes=[B("ebmm")])
    P.op("dve", _I.memset(zcar[:], 0.0), writes=[B("zcar")])
    P.op("dve", _I.memset(acar[:], 0.0), writes=[B("acar")])
    vxm3 = vxm[:].rearrange("p (a b) -> p a b", a=4)
    P.op("dve", _I.memset(vxm[:], 0.0), writes=[B("vx", "m")])
    P.op("dve", _I.memset(vxm3[32:33, :, 64:128], 1.0), writes=[B("vx", "m")])
    P.op("dve", _I.memset(vxm3[0:NMETA, :, 64:128], 1.0), writes=[B("vx", "m")])
    vx4 = vx[:].rearrange("p s (a b) -> p s a b", a=4)
    P.op("dve", _I.memset(vx[:], 1.0), writes=[B("vx", "L")] + [B("vx", s_) for s_ in range(maxpass)])
    P.op("dve", _I.memset(Pm[:], 0.0), writes=[B("Pm", kv) for kv in range(4)])
    P.op("dve", _I.memset(Pmm[:], 0.0), writes=[B("Pmm", kv) for kv in range(4)])

    xs_toggle = [0]
    scr_claim([B("stage", 0), B("stage", 1)])

    def load_tokens(src_ap, nrows, dst3, hbufs):
        i = xs_toggle[0] % 2
        xs_toggle[0] += 1
        P.op("sp", _I.dma_start(out=stage[0:nrows, i, :], in_=src_ap), writes=[B("stage", i)], dma=("xs", i))
        for half in range(2):
            bank = 2 * i + half
            for kk in range(4):
                k = half * 4 + kk
                P.op("pe", _I.transpose(
                    out=ps[bank][:, kk * 128:kk * 128 + nrows], in_=stage[0:nrows, i, k * 128:(k + 1) * 128],
                    identity=ident[0:nrows, 0:nrows]),
                    reads=[B("stage", i), B("ident")], writes=[psb[bank]], inc=(kk == 3))
            src = ps[bank][:].rearrange("p (a b) -> p a b", a=4)[:, :, 0:nrows]
            if half == 0:
                P.op("act", _I.copy(out=dst3[:, half * 4:half * 4 + 4, :], in_=src),
                     reads=[psb[bank]], writes=hbufs)
            else:
                P.op("dve", _I.tensor_copy(out=dst3[:, half * 4:half * 4 + 4, :], in_=src),
                     reads=[psb[bank]], writes=hbufs)

    load_tokens(meta_d, NMETA, hm[:], [B("hm")])
    for blk in range(NB):
        load_tokens(x_d[blk * 128:(blk + 1) * 128, :], 128, h[:, :, blk * 128:(blk + 1) * 128], [B("h", blk)])

    unit_ctr = [0]

    def load_unit(src_ap, nelem):
        u = unit_ctr[0]
        unit_ctr[0] += 1
        s = u % NSLOT
        dst = ring[:, s, 0:nelem]
        P.op("pool", _I.dma_start(out=dst, in_=src_ap), writes=[B("ring", s)], dma=("ring", s))
        return s

    class Tile:
        pass

    def scol(sl):
        return 0 if sl == "m" else (NMETA if sl == "L" else NMETA + 128 + sl * 128)

    def make_tiles(blocks, with_meta, left_block=None):
        tiles = []
        if with_meta:
            t = Tile()
            t.kind = "m"; t.c0 = 0; t.n = NMETA; t.blocks = []; t.slots = ["m"]
            t.hbufs = [B("hm")]
            t.hap = lambda k0, k1: hm[:, k0:k1, :]
            tiles.append(t)
        if left_block is not None:
            t = Tile()
            t.kind = "L"; t.c0 = NMETA; t.n = 128; t.blocks = [left_block]; t.slots = ["L"]
            t.hbufs = [B("h", left_block)]
            t.hap = lambda k0, k1, b=left_block: h[:, k0:k1, b * 128:(b + 1) * 128]
            tiles.append(t)
        pos = 0
        for nb_ in _split_tiles(len(blocks)):
            t = Tile()
            t.kind = "r"; t.c0 = NMETA + 128 + pos * 128; t.n = nb_ * 128
            t.blocks = blocks[pos:pos + nb_]; t.slots = list(range(pos, pos + nb_))
            t.hbufs = [B("h", b) for b in t.blocks]
            hc0 = t.blocks[0] * 128
            t.hap = lambda k0, k1, hc0=hc0, n=t.n: h[:, k0:k1, hc0:hc0 + n]
            tiles.append(t)
            pos += nb_
        for t in tiles:
            t.abufs = [B("abf", sl) for sl in t.slots]
            t.gb = lambda c, t=t: [B("gbf", c, sl) for sl in t.slots]
            t.ktb = [B("kt", sl) for sl in t.slots]
        return tiles

    sq_ctr = [0]
    nrm_ctr = [0]

    def rms_stats(hsrc, hb, n):
        ri = nrm_ctr[0] % 2
        nrm_ctr[0] += 1
        bank = 6 + ri
        for k in range(KC):
            si = sq_ctr[0] % 2
            sq_ctr[0] += 1
            P.op("act", _I.activation(out=sq[:, si, 0:n], in_=hsrc(k), func=AF.Square),
                 reads=hb, writes=[B("sq", si)])
            P.op("pe", _I.matmul(ps[bank][:, 0:n], lhsT=onesb[:], rhs=sq[:, si, 0:n],
                                                      start=(k == 0), stop=(k == KC - 1)),
                 reads=[B("sq", si), B("onesb")], writes=[psb[bank]], inc=True)
        P.op("act", _I.activation(out=rstd[:, ri, 0:n], in_=ps[bank][:, 0:n], func=AF.Ln, bias=epsb[:], scale=1.0),
             reads=[psb[bank], B("epsb")], writes=[B("rstd", ri)])
        P.op("act", _I.activation(out=rstd[:, ri, 0:n], in_=rstd[:, ri, 0:n], func=AF.Exp, scale=-0.5),
             reads=[B("rstd", ri)], writes=[B("rstd", ri)])
        return ri

    def tile_hsrc(t):
        return (lambda k: t.hap(k, k + 1)[:, 0, :]), t.hbufs, t.n

    def norm_to_abf(t, gcolap):
        hsrc, hb, n = tile_hsrc(t)
        ri = rms_stats(hsrc, hb, n)
        for k in range(KC):
            P.op("dve", _I.scalar_tensor_tensor(out=abf[:, k, t.c0:t.c0 + n], in0=hsrc(k), scalar=gcolap[:, k:k + 1],
                                                              in1=rstd[:, ri, 0:n], op0=ALU.mult, op1=ALU.mult),
                 reads=hb + [B("rstd", ri), B("gam")], writes=t.abufs)

    bank_ctr = [0]

    def next_bank(lo, hi):
        b = lo + bank_ctr[0] % (hi - lo)
        bank_ctr[0] += 1
        return b

    def h_add(t, m, bank, scalar_ap=None, mult=False):
        dst = t.hap(m, m + 1)[:, 0, :]
        if scalar_ap is None:
            P.op("dve", _I.tensor_tensor(out=dst, in0=ps[bank][:, 0:t.n], in1=dst, op=ALU.add),
                 reads=[psb[bank]] + t.hbufs, writes=t.hbufs)
        else:
            P.op("dve", _I.scalar_tensor_tensor(out=dst, in0=ps[bank][:, 0:t.n], scalar=scalar_ap, in1=dst,
                                                         op0=(ALU.mult if mult else ALU.add), op1=ALU.add),
                 reads=[psb[bank]] + t.hbufs, writes=t.hbufs)

    def proj(s, w_off, tiles, src3, src_bufs_fn, consume, nk=KC, wstride=128):
        for t in tiles:
            bank = next_bank(0, 6)
            for k in range(nk):
                P.op("pe", _I.matmul(
                    ps[bank][:, 0:t.n], lhsT=ring[:, s, w_off + k * wstride:w_off + k * wstride + 128],
                    rhs=src3[:, k, t.c0:t.c0 + t.n], start=(k == 0), stop=(k == nk - 1)),
                    reads=[B("ring", s)] + src_bufs_fn(t, k), writes=[psb[bank]], inc=(k == nk - 1))
            consume(t, bank)

    FGRP = ((0, 8), (8, 7), (15, 7))

    def ffn_pass(l, tiles, hoist=None):
        P.alias(ET_ALL, ABF_ALL)
        for t in tiles:
            norm_to_abf(t, gcol["ffn"](l))
        for g, (j0, nj) in enumerate(FGRP):
            for jj in range(nj):
                s = load_unit(gu_d[l, j0 + jj], 2 * KC * 128)
                for t in tiles:
                    bg = next_bank(0, 6)
                    bu = next_bank(0, 6)
                    for which, bank in ((0, bg), (1, bu)):
                        for k in range(KC):
                            off = which * KC * 128 + k * 128
                            P.op("pe", _I.matmul(
                                ps[bank][:, 0:t.n], lhsT=ring[:, s, off:off + 128], rhs=abf[:, k, t.c0:t.c0 + t.n],
                                start=(k == 0), stop=(k == KC - 1)),
                                reads=[B("ring", s)] + t.abufs, writes=[psb[bank]], inc=(k == KC - 1))
                    si = next_bank(0, 2)
                    P.op("act", _I.activation(out=sgt[:, si, 0:t.n], in_=ps[bg][:, 0:t.n], func=AF.Silu),
                         reads=[psb[bg]], writes=[B("sgt", si)])
                    P.op("dve", _I.tensor_tensor(
                        out=gbf[:, jj, t.c0:t.c0 + t.n], in0=ps[bu][:, 0:t.n], in1=sgt[:, si, 0:t.n], op=ALU.mult),
                        reads=[psb[bu], B("sgt", si)], writes=t.gb(jj))
            if g == len(FGRP) - 1 and hoist is not None:
                hoist()
            for m in range(KC):
                s = load_unit(dn_d[l, g, m][:, 0:nj * 128], nj * 128)
                for t in tiles:
                    bank = next_bank(0, 6)
                    for jj in range(nj):
                        P.op("pe", _I.matmul(
                            ps[bank][:, 0:t.n], lhsT=ring[:, s, jj * 128:(jj + 1) * 128], rhs=gbf[:, jj, t.c0:t.c0 + t.n],
                            start=(jj == 0), stop=(jj == nj - 1)),
                            reads=[B("ring", s)] + t.gb(jj), writes=[psb[bank]], inc=(jj == nj - 1))
                    h_add(t, m, bank)

    att_ctr = [0]
    ones_ok = {}

    def vslot(sl):
        return 0 if sl == "L" else sl + 1

    def qbufs(kv, sl):
        return [B("gbf", 2 * kv, sl), B("gbf", 2 * kv + 1, sl)]

    def bcast_q(ap2, nq):
        return bass.AP(ap2.tensor, ap2.offset, [list(ap2.ap[0]), list(ap2.ap[1]), [0, nq]])

    def att_A(qsl, qn, kv, chunks, kind):
        ai = att_ctr[0] % 2
        att_ctr[0] += 1
        qc0 = scol(qsl)
        W2 = 2 * qn
        W4 = 4 * qn
        sbanks = []
        for ci, ch in enumerate(chunks):
            nk = ch["nk"]
            kc0 = scol(ch["kt_sl"])
            bank = (ci if kind == "r" else 2) + 3 * ai
            sbanks.append(bank)
            for hi in range(2):
                P.op("pe", _I.matmul(
                    ps[bank][0:nk, hi * W2:(hi + 1) * W2].rearrange("p (a b) -> p a b", a=2),
                    lhsT=kt[:, 2 * kv + hi, kc0:kc0 + nk], rhs=qt[:, 2 * kv:2 * kv + 2, qc0:qc0 + qn], start=True, stop=True),
                    reads=[B("kt", ch["kt_sl"])] + qbufs(kv, qsl), writes=[psb[bank]], inc=(hi == 1))
        pv = []
        for ci, ch in enumerate(chunks):
            nk = ch["nk"]
            bank = sbanks[ci]
            if ch["band"]:
                e_ap = Et[:, ai, ci, 0:W4]; e_buf = B("Et", ai, ci)
                p_ap = Pt[:, ai, ci, 0:W4]; p_buf = B("Pt", ai, ci)
                full_p = p_ap
            elif kind == "r":
                e_ap = Em[0:NMETA, ai, 0:W4]; e_buf = B("Em", ai)
                p_ap = Pm[0:NMETA, kv, 0:W4]; p_buf = B("Pm", kv)
                full_p = Pm[0:33, kv, 0:W4]
            else:
                e_ap = Em[0:NMETA, ai, 0:W4]; e_buf = B("Em", ai)
                p_ap = Pmm[0:NMETA, kv, 0:W4]; p_buf = B("Pmm", kv)
                full_p = Pmm[0:33, kv, 0:W4]
            P.op("act", _I.activation(out=e_ap, in_=ps[bank][0:nk, 0:W4], func=AF.Exp, scale=0.125),
                 reads=[psb[bank]], writes=[e_buf])
            if ch.get("ebq"):
                in0 = e_ap.rearrange("p (s q) -> p s q", s=4)
                outp = p_ap.rearrange("p (s q) -> p s q", s=4)
                P.op("dve", _I.tensor_tensor(out=outp, in0=in0, in1=ch["eb_ap"], op=ALU.mult),
                     reads=[e_buf, B(ch["eb_key"])], writes=[p_buf])
            else:
                P.op("dve", _I.tensor_tensor(out=p_ap, in0=e_ap, in1=ch["eb_ap"], op=ALU.mult),
                     reads=[e_buf, B(ch["eb_key"])], writes=[p_buf])
            pv.append((ch["vx_ap"], ch["vx_buf"], full_p, p_buf))
        return dict(ai=ai, pv=pv, qc0=qc0, qn=qn, kv=kv, qsl=qsl, W2=W2, W4=W4)

    def att_B(c):
        ai, pv, qc0, qn, kv, qsl, W2, W4 = c["ai"], c["pv"], c["qc0"], c["qn"], c["kv"], c["qsl"], c["W2"], c["W4"]
        obank = 6 + ai
        for ci, (vx_ap, vx_buf, full_p, p_buf) in enumerate(pv):
            P.op("pe", _I.matmul(
                ps[obank][:, 0:W4], lhsT=vx_ap, rhs=full_p, start=(ci == 0), stop=(ci == len(pv) - 1)),
                reads=[vx_buf, p_buf], writes=[psb[obank]], inc=(ci == len(pv) - 1))
        P.op("act", _I.activation(out=rden[64:128, ai, 0:W4], in_=ps[obank][64:128, 0:W4], func=AF.Ln),
             reads=[psb[obank]], writes=[B("sgt", ai)])
        P.op("act", _I.activation(out=rden[64:128, ai, 0:W4], in_=rden[64:128, ai, 0:W4], func=AF.Exp, scale=-1.0),
             reads=[B("sgt", ai)], writes=[B("sgt", ai)])
        for hi in range(2):
            o_out = qt[hi * 64:(hi + 1) * 64, 2 * kv:2 * kv + 2, qc0:qc0 + qn]
            o_in = ps[obank][0:64, hi * W2:(hi + 1) * W2].rearrange("p (a b) -> p a b", a=2)
            r_in = rden[64:128, ai, hi * W2:(hi + 1) * W2].rearrange("p (a b) -> p a b", a=2)
            P.op("dve", _I.tensor_tensor(out=o_out, in0=o_in, in1=r_in, op=ALU.mult),
                 reads=[psb[obank], B("sgt", ai)], writes=qbufs(kv, qsl))

    def mixer_pre(l, pi, blocks, with_left):
        first = (pi == 0)
        left_block = blocks[0] - 1
        tiles = make_tiles(blocks, with_meta=first, left_block=(left_block if (first and with_left) else None))
        P.alias(ET_ALL, ABF_ALL)
        for t in tiles:
            norm_to_abf(t, gcol["mix"](l))
        return tiles

    def attn_pass(l, j, pi, blocks, tiles):
        first = (pi == 0)
        rtiles = [t for t in tiles if t.kind == "r"]
        qtiles = [t for t in tiles if t.kind != "L"]
        if first:
            scr_claim([B("ebmf")])
            P.op("sp", _I.dma_start(out=ebmf, in_=bmf_d), writes=[B("ebmf")], dma=("ebmf", l))
            P.op("act", _I.activation(out=ebmf, in_=ebmf, func=AF.Exp), reads=[B("ebmf")], writes=[B("ebmf")])
            srow = sinkst[32:33, j * 16:(j + 1) * 16]
            P.op("act", _I.activation(out=Pm[32:33, :, :].rearrange("p a (s q) -> p (a s) q", s=4),
                                               in_=bcast_q(srow, 128), func=AF.Exp),
                 reads=[B("sinkst", j)], writes=[B("Pm", kv) for kv in range(4)])
            P.op("act", _I.activation(out=Pmm[32:33, :, :].rearrange("p a (s q) -> p (a s) q", s=4),
                                               in_=bcast_q(srow, NMETA), func=AF.Exp),
                 reads=[B("sinkst", j)], writes=[B("Pmm", kv) for kv in range(4)])
        abf_src = lambda t, k: t.abufs
        for c in range(8):
            s = load_unit(aq_d[j, c], KC * 128)

            def cons(t, bank, c=c):
                P.op("act", _I.activation(out=qt[:, c, t.c0:t.c0 + t.n], in_=ps[bank][:, 0:t.n], func=AF.Identity,
                                                   bias=attb[:, j, c:c + 1], scale=1.0),
                     reads=[psb[bank], B("attb", j)], writes=t.gb(c))
            proj(s, 0, qtiles, abf, abf_src, cons)
        for c in range(8):
            s = load_unit(ak_d[j, c], KC * 128)

            def cons(t, bank, c=c):
                P.op("act", _I.activation(out=kt[:, c, t.c0:t.c0 + t.n], in_=ps[bank][:, 0:t.n], func=AF.Identity,
                                                   bias=attb[:, j, 8 + c:9 + c], scale=1.0),
                     reads=[psb[bank], B("attb", j)], writes=t.ktb)
            proj(s, 0, tiles, abf, abf_src, cons)
        s = load_unit(av_d[j], KC * 256)
        for t in tiles:
            for sbi, sl in enumerate(t.slots):
                if t.kind == "m":
                    rows = NMETA; dst = vxm3[0:NMETA, :, 0:64]; wb = [B("vx", "m")]; gbk = None
                else:
                    rows = 128
                    gbk = t.blocks[sbi]
                    dst = vx4[:, vslot(sl), :, 0:64]
                    wb = [B("vx", sl)]
                    if not ones_ok.get(vslot(sl), True):
                        P.op("dve", _I.memset(vx4[:, vslot(sl), :, 64:128], 1.0), writes=wb)
                        ones_ok[vslot(sl)] = True
                cc0 = scol(sl)
                bank = next_bank(0, 6)
                for k in range(KC):
                    P.op("pe", _I.matmul(
                        ps[bank][0:rows, 0:256], lhsT=abf[:, k, cc0:cc0 + rows], rhs=ring[:, s, k * 256:(k + 1) * 256],
                        start=(k == 0), stop=(k == KC - 1)),
                        reads=[B("ring", s), B("abf", sl)], writes=[psb[bank]], inc=(k == KC - 1))
                P.op("dve", _I.tensor_tensor(
                    out=dst, in0=ps[bank][0:rows, 0:256].rearrange("p (a b) -> p a b", a=4),
                    in1=attbv[0:rows, j, :].rearrange("p (a b) -> p a b", a=4), op=ALU.add),
                    reads=[psb[bank], B("attbv", j)], writes=wb)
                if gbk == NHALO - 1:
                    vs = vslot(sl)
                    P.op("dve", _I.tensor_scalar_mul(out=vx[:, vs, :], in0=vx[:, vs, :], scalar1=flags[:, 1:2]),
                         reads=wb + [B("flags")], writes=wb)
                    ones_ok[vs] = False
        P.alias(ABF_ALL, ET_ALL)

        def meta_chunk(kv, first_blk):
            d = dict(kt_sl="m", nk=NMETA, vx_ap=vxm[0:33, kv * 128:(kv + 1) * 128], vx_buf=B("vx", "m"), band=False)
            if first_blk == "mm":
                d.update(eb_ap=ebmm[:, kv * 64:(kv + 1) * 64], eb_key="ebmm")
            elif first_blk:
                d.update(eb_ap=ebmf[:, kv * 512:(kv + 1) * 512], eb_key="ebmf")
            else:
                d.update(eb_ap=bcast_q(ebmr[:, kv * 4:(kv + 1) * 4], 128), eb_key="ebmr", ebq=True)
            return d
        items = []
        if first:
            for kv in range(4):
                items.append(("m", NMETA, kv, [meta_chunk(kv, "mm")], "m"))
        for t in rtiles:
            for sbi, sl in enumerate(t.slots):
                gbk = t.blocks[sbi]
                psl = "L" if sl == 0 else sl - 1
                for kv in range(4):
                    chunks = [
                        dict(kt_sl=psl, nk=128, vx_ap=vx[:, vslot(psl), kv * 128:(kv + 1) * 128], vx_buf=B("vx", psl),
                             eb_ap=ebp[:, kv * 512:(kv + 1) * 512], eb_key="ebp", band=True),
                        dict(kt_sl=sl, nk=128, vx_ap=vx[:, vslot(sl), kv * 128:(kv + 1) * 128], vx_buf=B("vx", sl),
                             eb_ap=ebc[:, kv * 512:(kv + 1) * 512], eb_key="ebc", band=True),
                        meta_chunk(kv, gbk == NHALO),
                    ]
                    items.append((sl, 128, kv, chunks, "r"))
        pending = None
        for it in items:
            ctx = att_A(*it)
            if pending is not None:
                att_B(pending)
            pending = ctx
        if pending is not None:
            att_B(pending)
        for m in range(8):
            s = load_unit(ao_d[j, m], KC * 128)

            def cons(t, bank, m=m):
                h_add(t, m, bank, scalar_ap=attb[:, j, 16 + m:17 + m])
            proj(s, 0, qtiles, qt, lambda t, k: t.gb(k), cons)
        return tiles

    def attn_carry(blocks):
        lsl = len(blocks) - 1
        lc0 = scol(lsl)
        P.op("dve", _I.tensor_copy(out=kt[:, :, NMETA:NMETA + 128], in_=kt[:, :, lc0:lc0 + 128]),
             reads=[B("kt", lsl)], writes=[B("kt", "L")])
        P.op("dve", _I.tensor_copy(out=vx[:, 0, :], in_=vx[:, vslot(lsl), :]),
             reads=[B("vx", lsl)], writes=[B("vx", "L")])
        ones_ok[0] = ones_ok.get(vslot(lsl), True)

    def conv_pass(l, pi, blocks, tiles):
        first = (pi == 0)
        if first:
            scr_claim([B("zt", 0), B("zt", 1), B("cacc", 0), B("cacc", 1)])
        ybf = gbf
        for c in range(8):
            slots3 = [load_unit(ci_d[c, part], KC * 128) for part in range(3)]
            for t in tiles:
                banks = [next_bank(0, 6) for _ in range(3)]
                for part in range(3):
                    for k in range(KC):
                        P.op("pe", _I.matmul(
                            ps[banks[part]][:, 0:t.n], lhsT=ring[:, slots3[part], k * 128:(k + 1) * 128], rhs=abf[:, k, t.c0:t.c0 + t.n],
                            start=(k == 0), stop=(k == KC - 1)),
                            reads=[B("ring", slots3[part])] + t.abufs, writes=[psb[banks[part]]], inc=(k == KC - 1))
                bB, bC, bU = banks
                zi = next_bank(0, 2)
                n = t.n
                P.op("act", _I.copy(out=sgt[:, zi, 0:n], in_=ps[bU][:, 0:n]),
                     reads=[psb[bU]], writes=[B("sgt", zi)])
                if t.kind == "m":
                    P.op("dve", _I.memset(zt[:, zi, 0:2], 0.0), writes=[B("zt", zi)])
                else:
                    P.op("dve", _I.tensor_copy(out=zt[:, zi, 0:2], in_=zcar[:, c, :]),
                         reads=[B("zcar")], writes=[B("zt", zi)])
                P.op("dve", _I.tensor_tensor(out=zt[:, zi, 2:2 + n], in0=ps[bC][:, 0:n], in1=sgt[:, zi, 0:n], op=ALU.mult),
                     reads=[psb[bC], B("sgt", zi)], writes=[B("zt", zi)])
                if t.kind != "m":
                    P.op("dve", _I.tensor_copy(out=zcar[:, c, :], in_=zt[:, zi, n:n + 2]),
                         reads=[B("zt", zi)], writes=[B("zcar")])
                P.op("dve", _I.tensor_scalar_mul(out=cacc[:, zi, 0:n], in0=zt[:, zi, 0:n], scalar1=cw[:, c:c + 1]),
                     reads=[B("zt", zi), B("cw")], writes=[B("cacc", zi)])
                for tap in (1, 2):
                    P.op("dve", _I.scalar_tensor_tensor(
                        out=cacc[:, zi, 0:n], in0=zt[:, zi, tap:tap + n], scalar=cw[:, tap * KC + c:tap * KC + c + 1],
                        in1=cacc[:, zi, 0:n], op0=ALU.mult, op1=ALU.add),
                        reads=[B("zt", zi), B("cw"), B("cacc", zi)], writes=[B("cacc", zi)])
                P.op("dve", _I.tensor_tensor(out=ybf[:, c, t.c0:t.c0 + n], in0=ps[bB][:, 0:n],
                                                                                  in1=cacc[:, zi, 0:n], op=ALU.mult),
                     reads=[psb[bB], B("cacc", zi)], writes=t.gb(c))
        for m in range(8):
            s = load_unit(co_d[m], KC * 128)

            def cons(t, bank, m=m):
                h_add(t, m, bank)
            proj(s, 0, tiles, ybf, lambda t, k: t.gb(k), cons)
        return tiles

    def pool_pass(l, pi, blocks):
        first = (pi == 0)
        tiles = make_tiles(blocks, with_meta=first)
        mixbf = gbf
        if first:
            scr_claim([B("afk", 0), B("afk", 1)] + [B("pls", i) for i in range(3)])
        s = load_unit(pw_d, 4 * 2 * 256)
        seen_real = False
        for t in tiles:
            n = t.n
            if t.kind == "m":
                hsrc, hb, nn = tile_hsrc(t)
                mode = "zero"; off = NMETA
            elif first and not seen_real:
                hc0 = t.blocks[0] * 128 - NMETA
                nn = n + NMETA
                hsrc = lambda k, hc0=hc0, nn=nn: h[:, k, hc0:hc0 + nn]
                hb = t.hbufs + [B("h", t.blocks[0] - 1)]
                mode = "own"; off = 0
            else:
                hsrc, hb, nn = tile_hsrc(t)
                mode = "carry"; off = NMETA
            ri = rms_stats(hsrc, hb, nn)
            W = NMETA + n
            for k in range(KC):
                ki = k % 2
                if mode == "zero":
                    P.op("dve", _I.memset(afk[:, ki, 0:NMETA], 0.0), writes=[B("afk", ki)])
                elif mode == "carry":
                    P.op("dve", _I.tensor_copy(out=afk[:, ki, 0:NMETA], in_=acar[:, k, :]),
                         reads=[B("acar")], writes=[B("afk", ki)])
                P.op("dve", _I.scalar_tensor_tensor(
                    out=afk[:, ki, off:off + nn], in0=hsrc(k), scalar=gcol["mix"](l)[:, k:k + 1], in1=rstd[:, ri, 0:nn],
                    op0=ALU.mult, op1=ALU.mult),
                    reads=hb + [B("rstd", ri), B("gam")], writes=[B("afk", ki)])
                if t.kind != "m":
                    P.op("dve", _I.tensor_copy(out=acar[:, k, :], in_=afk[:, ki, n:n + NMETA]),
                         reads=[B("afk", ki)], writes=[B("acar")])
                w = POOL_W[k]
                step = 1
                si = 0
                while step < w:
                    dsti = si % 3
                    lo = 2 * step - 1
                    if step == 1:
                        in0 = afk[:, ki, lo:W]; in1 = afk[:, ki, lo - step:W - step]; rb = [B("afk", ki)]
                    else:
                        prev = (si - 1) % 3
                        in0 = pls[:, prev, lo:W]; in1 = pls[:, prev, lo - step:W - step]; rb = [B("pls", prev)]
                    P.op("dve", _I.tensor_tensor(out=pls[:, dsti, lo:W], in0=in0, in1=in1, op=ALU.add),
                         reads=rb, writes=[B("pls", dsti)])
                    step *= 2
                    si += 1
                last = (si - 1) % 3
                if t.kind == "m":
                    P.op("dve", _I.tensor_tensor(out=pls[:, last, NMETA:W], in0=pls[:, last, NMETA:W],
                                                                           in1=pic[:, k * NMETA:(k + 1) * NMETA], op=ALU.mult),
                         reads=[B("pls", last), B("pic")], writes=[B("pls", last)])
                    P.op("dve", _I.tensor_tensor(out=mixbf[:, k, t.c0:t.c0 + n], in0=pls[:, last, NMETA:W],
                                                                                       in1=afk[:, ki, NMETA:W], op=ALU.subtract),
                         reads=[B("pls", last), B("afk", ki)], writes=t.gb(k))
                else:
                    P.op("dve", _I.scalar_tensor_tensor(
                        out=mixbf[:, k, t.c0:t.c0 + n], in0=pls[:, last, NMETA:W], scalar=1.0 / w, in1=afk[:, ki, NMETA:W],
                        op0=ALU.mult, op1=ALU.subtract),
                        reads=[B("pls", last), B("afk", ki)], writes=t.gb(k))
            if t.kind != "m":
                seen_real = True
            for g in range(4):
                for dc in range(2):
                    m = 2 * g + dc
                    bank = next_bank(0, 6)
                    for cc in range(2):
                        off2 = (g * 2 + cc) * 256 + dc * 128
                        P.op("pe", _I.matmul(
                            ps[bank][:, 0:t.n], lhsT=ring[:, s, off2:off2 + 128], rhs=mixbf[:, 2 * g + cc, t.c0:t.c0 + t.n],
                            start=(cc == 0), stop=(cc == 1)),
                            reads=[B("ring", s)] + t.gb(2 * g + cc), writes=[psb[bank]], inc=(cc == 1))
                    h_add(t, m, bank, scalar_ap=psc[:, m:m + 1], mult=True)
        return tiles

    def blend():
        c0 = NHALO * 128 - NMETA
        P.op("dve", _I.tensor_scalar_mul(out=tmpb[:], in0=hm[:], scalar1=flags[:, 0:1]),
             reads=[B("hm"), B("flags")], writes=[B("tmpb")])
        P.op("dve", _I.scalar_tensor_tensor(out=h[:, :, c0:c0 + NMETA], in0=h[:, :, c0:c0 + NMETA], scalar=flags[:, 1:2],
                                                     in1=tmpb[:], op0=ALU.mult, op1=ALU.add),
             reads=[B("tmpb"), B("h", NHALO - 1), B("flags")], writes=[B("h", NHALO - 1)])

    q0s = [1, 1, 2, 3]
    steps = []
    for l in range(n_layers):
        q0 = min(q0s[l], NB - 1)
        for pi, blocks in enumerate(_passes(q0, NB)):
            steps.append((l, pi, blocks))

    def step_pre(si):
        l, pi, blocks = steps[si]
        kind = l % 3
        if pi == 0 and kind in (1, 2):
            blend()
        if kind == 2:
            return None
        return mixer_pre(l, pi, blocks, with_left=(kind == 0))

    pre_tiles = step_pre(0) if steps else None
    for si, (l, pi, blocks) in enumerate(steps):
        kind, j = l % 3, l // 3
        npass = sum(1 for st in steps if st[0] == l)
        if kind == 0:
            tiles = attn_pass(l, j, pi, blocks, pre_tiles)
            if pi + 1 < npass:
                attn_carry(blocks)
        elif kind == 1:
            tiles = conv_pass(l, pi, blocks, pre_tiles)
        else:
            tiles = pool_pass(l, pi, blocks)
        nxt = {}

        def hoist(si=si, nxt=nxt):
            nxt["tiles"] = step_pre(si + 1)
        can_hoist = False
        if si + 1 < len(steps):
            nl_, npi, nblocks = steps[si + 1]
            touched = set(nblocks) | {nblocks[0] - 1, nblocks[0] - 2}
            can_hoist = not (touched & set(blocks)) and not (npi == 0 and pi == 0)
        ffn_pass(l, [t for t in tiles if t.kind != "L"], hoist if can_hoist else None)
        if si + 1 < len(steps) and not can_hoist:
            hoist()
        pre_tiles = nxt.get("tiles")

    out_ctr = [0]
    scr_claim([B("stage", 0), B("stage", 1)])
    P.alias(ABF_ALL + ET_ALL, [B("yf", 0), B("yf", 1)])

    def emit_rows(src_fn, src_bufs, nrows, row0):
        oi = out_ctr[0] % 2
        out_ctr[0] += 1
        for half in range(2):
            bank = 2 * oi + half
            for kk in range(4):
                k = half * 4 + kk
                P.op("pe", _I.transpose(
                    out=ps[bank][0:nrows, kk * 128:(kk + 1) * 128], in_=src_fn(k), identity=ident[:]),
                    reads=src_bufs + [B("ident")], writes=[psb[bank]], inc=(kk == 3))
            if half == 0:
                P.op("act", _I.copy(out=stage[0:nrows, oi, 0:512], in_=ps[bank][0:nrows, :]),
                     reads=[psb[bank]], writes=[B("stage", oi)])
            else:
                P.op("dve", _I.tensor_copy(out=stage[0:nrows, oi, 512:1024], in_=ps[bank][0:nrows, :]),
                     reads=[psb[bank]], writes=[B("stage", oi)])
        P.op("sp", _I.dma_start(out=out_d[row0:row0 + nrows, :], in_=stage[0:nrows, oi, :]),
             reads=[B("stage", oi)], dma=("outq", oi))

    if debug_h:
        for bi in range(n_own):
            blk = NHALO + bi
            emit_rows(lambda k, blk=blk: h[:, k, blk * 128:(blk + 1) * 128], [B("h", blk)], 128, bi * 128)
        emit_rows(lambda k: hm[:, k, :], [B("hm")], NMETA, n_own * 128)
    else:
        for bi in range(n_own):
            blk = NHALO + bi
            hsrc = lambda k, blk=blk: h[:, k, blk * 128:(blk + 1) * 128]
            hb = [B("h", blk)]
            ri = rms_stats(hsrc, hb, 128)
            yi = bi % 2
            for k in range(KC):
                P.op("dve", _I.scalar_tensor_tensor(
                    out=yfv[yi][:, k, :], in0=hsrc(k), scalar=gcol["fin"](0)[:, k:k + 1], in1=rstd[:, ri, 0:128],
                    op0=ALU.mult, op1=ALU.mult),
                    reads=hb + [B("rstd", ri), B("gam")], writes=[B("yf", yi)])
            emit_rows(lambda k, yi=yi: yfv[yi][:, k, :], [B("yf", yi)], 128, bi * 128)

    P.emit(nc, final_waits=tuple(k for k in (("outq", 0), ("outq", 1)) if k in P.cnt))
    return nc


def _prep_shared(inp):
    f = lambda a: np.ascontiguousarray(np.asarray(a, dtype=np.float32))
    sh = {}
    sh["meta"] = f(inp["meta_tokens"])
    sh["ident"] = np.eye(128, dtype=np.float32)
    gam = np.concatenate([_fm(f(inp["norm_mix"])).reshape(128, 4 * KC), _fm(f(inp["norm_ffn"])).reshape(128, 4 * KC),
                          _fm(f(inp["norm_final"])).reshape(128, KC)], axis=1)
    sh["gammas"] = np.ascontiguousarray(gam)
    wg, wu, wd = f(inp["ffn_w_gate"]), f(inp["ffn_w_up"]), f(inp["ffn_w_down"])
    gu = np.empty((4, FC, 128, 2, KC, 128), np.float32)
    dn = np.zeros((4, 3, KC, 128, 8, 128), np.float32)
    for l in range(4):
        gu[l, :, :, 0] = _arr(wg[l])
        gu[l, :, :, 1] = _arr(wu[l])
        a = _arr(wd[l])
        dn[l, 0, :, :, 0:8] = a[:, :, 0:8, :]
        dn[l, 1, :, :, 0:7] = a[:, :, 8:15, :]
        dn[l, 2, :, :, 0:7] = a[:, :, 15:22, :]
    sh["ffn_gu"] = gu.reshape(4, FC, 128, 2 * KC * 128)
    sh["ffn_dn"] = dn.reshape(4, 3, KC, 128, 8 * 128)
    wqkv, bqkv = f(inp["attn_w_qkv"]), f(inp["attn_b_qkv"])
    wo, bo = f(inp["attn_w_o"]), f(inp["attn_b_o"])
    aq = np.empty((2, 8, 128, KC, 128), np.float32)
    ak = np.zeros((2, 8, 128, KC, 128), np.float32)
    av = np.empty((2, 128, KC, 256), np.float32)
    ao = np.empty((2, 8, 128, KC, 128), np.float32)
    ab = np.zeros((2, 128, 24), np.float32)
    abv = np.empty((2, 128, 256), np.float32)
    for j in range(2):
        aq[j] = _arr(wqkv[j][:, 0:1024])
        wk = wqkv[j][:, 1024:1280]
        bk = bqkv[j][1024:1280]
        for kv in range(4):
            wkk = wk[:, kv * 64:(kv + 1) * 64].reshape(KC, 128, 64).transpose(1, 0, 2)
            ak[j, 2 * kv, :, :, 0:64] = wkk
            ak[j, 2 * kv + 1, :, :, 64:128] = wkk
            ab[j, 0:64, 8 + 2 * kv] = bk[kv * 64:(kv + 1) * 64]
            ab[j, 64:128, 8 + 2 * kv + 1] = bk[kv * 64:(kv + 1) * 64]
        av[j] = wqkv[j][:, 1280:1536].reshape(KC, 128, 256).transpose(1, 0, 2)
        ao[j] = _arr(wo[j])
        ab[j, :, 0:8] = _fm(bqkv[j][0:1024])
        ab[j, :, 16:24] = _fm(bo[j])
        abv[j] = np.broadcast_to(bqkv[j][1280:1536], (128, 256))
    sh["attn_q"] = aq.reshape(2, 8, 128, KC * 128)
    sh["attn_k"] = ak.reshape(2, 8, 128, KC * 128)
    sh["attn_v"] = av.reshape(2, 128, KC * 256)
    sh["attn_o"] = ao.reshape(2, 8, 128, KC * 128)
    sh["attn_b"] = ab
    sh["attn_bv"] = abv
    sinks = f(inp["attn_sinks"])
    sk = np.empty((2, 4, 4), np.float32)
    for j in range(2):
        for kv in range(4):
            for s_, g in enumerate(SLOT_ORDER):
                sk[j, kv, s_] = sinks[j, 4 * kv + g]
    sh["attn_sink"] = sk.reshape(2, 16)
    tab = f(inp["rel_bias_table"])
    bt = _bucket_table()
    kk = np.arange(128)[:, None]
    qq = np.arange(128)[None, :]
    d_prev = 128 + qq - kk
    d_cur = qq - kk
    m_prev = ((d_prev >= 0) & (d_prev < 128)).astype(np.float32)
    m_cur = ((d_cur >= 0) & (d_cur < 128)).astype(np.float32)
    bp = np.empty((128, 4, 4, 128), np.float32)
    bc = np.empty((128, 4, 4, 128), np.float32)
    for kv in range(4):
        for s_, g in enumerate(SLOT_ORDER):
            hh = 4 * kv + g
            bp[:, kv, s_, :] = tab[bt[np.clip(d_prev, 0, 511)], hh]
            bc[:, kv, s_, :] = tab[bt[np.clip(d_cur, 0, 511)], hh]
    sh["bias_prev"] = bp.reshape(128, 2048)
    sh["bias_cur"] = bc.reshape(128, 2048)
    sh["mask_prev"] = np.ascontiguousarray(np.broadcast_to(m_prev[:, None, None, :], (128, 4, 4, 128))).reshape(128, 2048)
    sh["mask_cur"] = np.ascontiguousarray(np.broadcast_to(m_cur[:, None, None, :], (128, 4, 4, 128))).reshape(128, 2048)
    mm_ = np.arange(NMETA)[:, None]
    d_first = NMETA + qq - mm_
    bmf = np.empty((NMETA, 4, 4, 128), np.float32)
    bmr = np.empty((NMETA, 4, 4), np.float32)
    mq = np.arange(NMETA)[None, :]
    d_mm = mq - mm_
    bmm = np.empty((NMETA, 4, 4, NMETA), np.float32)
    for kv in range(4):
        for s_, g in enumerate(SLOT_ORDER):
            hh = 4 * kv + g
            bmf[:, kv, s_, :] = tab[bt[d_first], hh]
            bmr[:, kv, s_] = tab[31, hh]
            bmm[:, kv, s_, :] = tab[bt[np.clip(d_mm, 0, 511)], hh]
    sh["_bmf"] = bmf.reshape(NMETA, 2048)
    sh["bias_meta_rest"] = bmr.reshape(NMETA, 16)
    sh["_bmr_full"] = np.ascontiguousarray(np.broadcast_to(bmr[:, :, :, None], (NMETA, 4, 4, 128))).reshape(NMETA, 2048)
    sh["bias_mm"] = bmm.reshape(NMETA, 256)
    sh["mask_mm"] = np.ascontiguousarray(np.broadcast_to((d_mm >= 0).astype(np.float32)[:, None, None, :], (NMETA, 4, 4, NMETA))).reshape(NMETA, 256)
    cin = f(inp["conv_w_in"])[0]
    ci = np.empty((8, 3, 128, KC, 128), np.float32)
    for part in range(3):
        ci[:, part] = _arr(cin[:, part * 1024:(part + 1) * 1024])
    sh["conv_in"] = ci.reshape(8, 3, 128, KC * 128)
    sh["conv_out"] = _arr(f(inp["conv_w_out"])[0]).reshape(8, 128, KC * 128)
    sh["conv_w"] = _fm(f(inp["conv_w"])[0]).reshape(128, 3 * KC)
    pw = f(inp["pool_w"])[0]
    sh["pool_w"] = np.ascontiguousarray(pw.reshape(4, 2, 128, 256).transpose(2, 0, 1, 3)).reshape(128, 2048)
    sh["pool_scale"] = _fm(f(inp["pool_scale"])[0]).reshape(128, KC)
    invc = np.empty((128, KC, NMETA), np.float32)
    tt = np.arange(NMETA)
    for k in range(KC):
        invc[:, k, :] = (1.0 / np.minimum(POOL_W[k], tt + 1)).astype(np.float32)[None, :]
    sh["pool_invc"] = invc.reshape(128, KC * NMETA)
    return sh


def _core_inputs(x, sh, c, n_own):
    b, ch = divmod(c, SEQ // CHUNK)
    s0 = ch * CHUNK
    T = (NHALO + n_own) * 128
    xc = np.zeros((T, D), np.float32)
    lo = s0 - NHALO * 128
    src_lo = max(lo, 0)
    hi = s0 + n_own * 128
    xc[src_lo - lo:, :] = x[b, src_lo:hi, :]
    m = {k: v for k, v in sh.items() if not k.startswith("_")}
    m["x"] = xc
    fl = np.zeros((128, 2), np.float32)
    fl[:, 0] = 1.0 if ch == 0 else 0.0
    fl[:, 1] = 0.0 if ch == 0 else 1.0
    m["flags"] = fl
    m["bias_meta_first"] = sh["_bmf"] if ch == 0 else sh["_bmr_full"]
    return m


_NC_CACHE = {}


def kernel(x, meta_tokens, rel_bias_table, norm_mix, norm_ffn, norm_final,
           attn_w_qkv, attn_b_qkv, attn_w_o, attn_b_o, attn_sinks,
           conv_w_in, conv_w, conv_w_out, pool_w, pool_scale,
           ffn_w_gate, ffn_w_up, ffn_w_down):
    inp = dict(meta_tokens=meta_tokens, rel_bias_table=rel_bias_table, norm_mix=norm_mix, norm_ffn=norm_ffn,
               norm_final=norm_final, attn_w_qkv=attn_w_qkv, attn_b_qkv=attn_b_qkv, attn_w_o=attn_w_o,
               attn_b_o=attn_b_o, attn_sinks=attn_sinks, conv_w_in=conv_w_in, conv_w=conv_w, conv_w_out=conv_w_out,
               pool_w=pool_w, pool_scale=pool_scale, ffn_w_gate=ffn_w_gate, ffn_w_up=ffn_w_up, ffn_w_down=ffn_w_down)
    x = np.ascontiguousarray(np.asarray(x, dtype=np.float32))
    sh = _prep_shared(inp)
    n_own = CHUNK // 128
    if "nc" not in _NC_CACHE:
        _NC_CACHE["nc"] = build(n_own=n_own)
    nc = _NC_CACHE["nc"]
    in_maps = [_core_inputs(x, sh, c, n_own) for c in range(N_CORES)]
    res = run_bass_kernel_spmd(nc, in_maps, core_ids=list(range(N_CORES)))
    out = np.empty((x.shape[0], SEQ, D), np.float32)
    for c in range(N_CORES):
        b, ch = divmod(c, SEQ // CHUNK)
        out[b, ch * CHUNK:(ch + 1) * CHUNK, :] = res.results[c]["out"]
    return out
```

```python
import os
import numpy as np
import concourse.bass as bass
import concourse.mybir as mybir
from concourse.bass_utils import run_bass_kernel_spmd

F32 = mybir.dt.float32
BF16 = mybir.dt.bfloat16
AF = mybir.ActivationFunctionType
ALU = mybir.AluOpType

D = 1024
KC = 8
DFF = 2816
FC = 22
NMETA = 16
NHALO = 3
EPS = 1e-6
N_CORES = 8
SEQ = 8192
CHUNK = 2048
POOL_W = (2, 2, 4, 4, 8, 8, 16, 16)
SLOT_ORDER = (0, 2, 1, 3)
RING_ELEMS = 2048
NSLOT = 6


class _Rec:
    def __getattr__(self, name):
        def f(*a, **kw):
            return (name, a, kw)
        return f


_I = _Rec()


class Buf:
    __slots__ = ("lw", "rd")

    def __init__(self):
        self.lw = None
        self.rd = []


class Prog:
    ENGS = ("pe", "act", "dve", "pool", "sp")

    def __init__(self):
        self.q = {e: [] for e in self.ENGS}
        self.cnt = {e: 0 for e in self.ENGS}
        self.seen = {e: {} for e in self.ENGS}
        self.bufs = {}

    def buf(self, *key):
        b = self.bufs.get(key)
        if b is None:
            b = self.bufs[key] = Buf()
        return b

    def op(self, eng, fn, reads=(), writes=(), inc=True, dma=None):
        raw = {}
        war = {}

        def need(d, tok):
            if tok is None:
                return
            k, v = tok
            if d.get(k, 0) < v:
                d[k] = v

        for b in reads:
            need(raw, b.lw)
        for b in writes:
            need(raw, b.lw)
            for r in b.rd:
                need(war, r)
        waits = []
        seen = self.seen[eng]
        for d, is_war in ((raw, False), (war, True)):
            for k, v in d.items():
                if k == eng and (eng == "pe" or is_war or dma is not None):
                    continue
                if seen.get(k, 0) < v:
                    seen[k] = v
                    waits.append((k, v))
        if dma is not None:
            self.cnt[dma] = self.cnt.get(dma, 0) + 16
            tok = (dma, self.cnt[dma])
            incinfo = (dma, 16)
        elif inc:
            self.cnt[eng] += 1
            tok = (eng, self.cnt[eng])
            incinfo = (eng, 1)
        else:
            tok = (eng, self.cnt[eng] + 1)
            incinfo = None
        for b in reads:
            b.rd.append(tok)
        for b in writes:
            b.lw = tok
            b.rd = []
        self.q[eng].append((waits, fn, incinfo))
        return tok

    def alias(self, old_bufs, new_bufs):
        toks = []
        for o in old_bufs:
            if o.lw is not None:
                toks.append(o.lw)
            toks.extend(o.rd)
        for n in new_bufs:
            n.rd.extend(toks)

    def emit(self, nc, final_waits):
        sems = {}
        for k in list(self.cnt.keys()):
            sems[k] = nc.alloc_semaphore("s_" + str(k).replace(" ", "").replace("'", "").replace("(", "").replace(")", "").replace(",", "_"))
        qmap = {"pe": "tensor", "act": "scalar", "dve": "vector", "pool": "gpsimd", "sp": "sync"}
        with nc.Block() as block:
            for eng in self.ENGS:
                lst = self.q[eng]
                fw = final_waits if eng == "sp" else ()

                def body(e, lst=lst, fw=fw):
                    for waits, fn, incinfo in lst:
                        for k, v in waits:
                            e.wait_ge(sems[k], v)
                        name, a_, kw_ = fn
                        ins = getattr(e, name)(*a_, **kw_)
                        if incinfo is not None:
                            ins.then_inc(sems[incinfo[0]], incinfo[1])
                    for k in fw:
                        e.wait_ge(sems[k], self.cnt[k])

                getattr(block, qmap[eng])(body)


def _arr(W):
    K, M = W.shape
    return np.ascontiguousarray(W.reshape(K // 128, 128, M // 128, 128).transpose(2, 1, 0, 3))


def _fm(v):
    lead = v.shape[:-1]
    r = v.reshape(lead + (KC, 128))
    r = np.moveaxis(r, -1, 0)
    return np.ascontiguousarray(r)


def _rel_bucket_np(d):
    d = np.maximum(d, 0)
    df = np.maximum(d, 1).astype(np.float32)
    large = 16 + (np.log(df / np.float32(16)) / np.float32(np.log(128 / 16)) * np.float32(16)).astype(np.int32)
    large = np.minimum(large, 31)
    return np.where(d < 16, d, large)


def _bucket_table():
    return _rel_bucket_np(np.arange(0, 512))


def _passes(q0, nb):
    blocks = list(range(q0, nb))
    n = len(blocks)
    npass = 3 if n >= 3 else 1
    base, rem = divmod(n, npass)
    out = []
    i = 0
    for p in range(npass):
        c = base + (1 if p < rem else 0)
        out.append(blocks[i:i + c])
        i += c
    return out


def _split_tiles(nblk):
    nt = (nblk + 3) // 4
    base, rem = divmod(nblk, nt)
    return [base + (1 if i < rem else 0) for i in range(nt)]


def build(n_own=16, n_layers=4, debug_h=False):
    NB = NHALO + n_own
    T = NB * 128
    nc = bass.Bass("TRN2", target_bir_lowering=False)
    P = Prog()
    B = P.buf

    def din(name, shape):
        return nc.dram_tensor(name, list(shape), F32, kind="ExternalInput").ap()

    x_d = din("x", [T, D])
    meta_d = din("meta", [NMETA, D])
    ident_d = din("ident", [128, 128])
    flag_d = din("flags", [128, 2])
    gam_d = din("gammas", [128, 9 * KC])
    gu_d = din("ffn_gu", [4, FC, 128, 2 * KC * 128])
    dn_d = din("ffn_dn", [4, 3, KC, 128, 8 * 128])
    aq_d = din("attn_q", [2, 8, 128, KC * 128])
    ak_d = din("attn_k", [2, 8, 128, KC * 128])
    av_d = din("attn_v", [2, 128, KC * 256])
    ao_d = din("attn_o", [2, 8, 128, KC * 128])
    ab_d = din("attn_b", [2, 128, 24])
    abv_d = din("attn_bv", [2, 128, 256])
    sink_d = din("attn_sink", [2, 16])
    ebp_d = din("bias_prev", [128, 4 * 512])
    ebc_d = din("bias_cur", [128, 4 * 512])
    mkp_d = din("mask_prev", [128, 4 * 512])
    mkc_d = din("mask_cur", [128, 4 * 512])
    bmf_d = din("bias_meta_first", [NMETA, 4 * 512])
    bmr_d = din("bias_meta_rest", [NMETA, 16])
    bmm_d = din("bias_mm", [NMETA, 4 * 64])
    mmm_d = din("mask_mm", [NMETA, 4 * 64])
    ci_d = din("conv_in", [8, 3, 128, KC * 128])
    co_d = din("conv_out", [8, 128, KC * 128])
    cw_d = din("conv_w", [128, 3 * KC])
    pw_d = din("pool_w", [128, 4 * 2 * 256])
    psc_d = din("pool_scale", [128, KC])
    pic_d = din("pool_invc", [128, KC * NMETA])
    out_rows = n_own * 128 + (128 if debug_h else 0)
    out_d = nc.dram_tensor("out", [out_rows, D], F32, kind="ExternalOutput").ap()

    def sb(name, shape, dt=F32):
        return nc.alloc_sbuf_tensor("sb_" + name, list(shape), dt)

    h = sb("h", [128, KC, T])
    hm = sb("hm", [128, KC, NMETA])
    maxpass = max(len(p) for q0 in (1, 2, 3) for p in _passes(min(q0, NB - 1), NB))
    NTP = max(NMETA + 128 + maxpass * 128, 768)
    abf_raw = sb("abf", [128, KC * NTP], BF16)
    abf = abf_raw[:].rearrange("p (k t) -> p k t", k=KC)
    Et = abf_raw[:, 0:4096].bitcast(F32).rearrange("p (a c w) -> p a c w", a=2, c=2)
    Em = abf_raw[:, 4096:6144].bitcast(F32).rearrange("p (a w) -> p a w", a=2)
    gbf = sb("gbf", [128, 8, NTP], BF16)
    qt = gbf
    kt = sb("kt", [128, 8, NTP], BF16)
    vx = sb("vx", [128, maxpass + 1, 4 * 128], BF16)
    vxm = sb("vxm", [64, 4 * 128], BF16)
    ring = sb("ring", [128, NSLOT, RING_ELEMS], BF16)
    ident = sb("ident", [128, 128])
    onesb = sb("onesb", [128, 128], BF16)
    flags = sb("flags", [128, 2])
    gam = sb("gam", [128, 9 * KC])
    epsb = sb("epsb", [128, 1])
    ebp = sb("ebp", [128, 4 * 512])
    ebc = sb("ebc", [128, 4 * 512])
    ebmr = sb("ebmr", [NMETA, 16])
    ebmm = sb("ebmm", [NMETA, 4 * 64])
    attb = sb("attb", [128, 2, 24])
    attbv = sb("attbv", [128, 2, 256])
    sinkst = sb("sinkst", [64, 2 * 16])
    cw = sb("cw", [128, 3 * KC])
    psc = sb("psc", [128, KC])
    pic = sb("pic", [128, KC * NMETA])
    sq = sb("sq", [128, 2, 512], BF16)
    rstd = sb("rstd", [128, 2, 512])
    sgt = sb("sgt", [128, 2, 512])
    rden = sgt
    SCR = 3072
    scr = sb("scr", [128, SCR])
    stage = scr[:, 1024:3072].rearrange("p (a w) -> p a w", a=2)
    mstage = scr[:, 0:2048]
    ebmf = scr[0:NMETA, 0:2048]
    zt = scr[:, 0:1028].rearrange("p (a w) -> p a w", a=2)
    cacc = scr[:, 1028:2052].rearrange("p (a w) -> p a w", a=2)
    afk = scr[:, 0:1056].rearrange("p (a w) -> p a w", a=2)
    pls = scr[:, 1056:2640].rearrange("p (a w) -> p a w", a=3)
    yfv = [abf_raw[:, i_ * 2048:(i_ + 1) * 2048].bitcast(F32).rearrange("p (k w) -> p k w", k=KC) for i_ in range(2)]
    Pt = sb("Pt", [128, 2, 2, 512], BF16)
    Pm = sb("Pm", [64, 4, 512], BF16)
    Pmm = sb("Pmm", [64, 4, 64], BF16)
    zcar = sb("zcar", [128, KC, 2])
    acar = sb("acar", [128, KC, NMETA])
    tmpb = sb("tmpb", [128, KC, NMETA])
    ps = [nc.alloc_psum_tensor("ps%d" % i, [128, 512], F32) for i in range(8)]
    psb = [B("ps", i) for i in range(8)]

    gcol = {"mix": lambda l: gam[:, l * KC:(l + 1) * KC], "ffn": lambda l: gam[:, (4 + l) * KC:(5 + l) * KC],
            "fin": lambda l: gam[:, 8 * KC:9 * KC]}

    scr_owner = [[]]

    def scr_claim(bufs):
        P.alias(scr_owner[0], bufs)
        scr_owner[0] = list(bufs)

    ABF_ALL = [B("abf", "m"), B("abf", "L")] + [B("abf", i) for i in range(maxpass)]
    ET_ALL = [B("Et", a_, c_) for a_ in range(2) for c_ in range(2)] + [B("Em", 0), B("Em", 1)]

    setup_bufs = []

    def sload(dst, src, *key):
        b = B(*key)
        setup_bufs.append(b)
        P.op("sp", _I.dma_start(out=dst, in_=src), writes=[b], dma="setup")
        return b

    sload(ident[:], ident_d, "ident")
    sload(flags[:], flag_d, "flags")
    sload(gam[:], gam_d, "gam")
    sload(ebp[:], ebp_d, "ebp")
    sload(ebc[:], ebc_d, "ebc")
    sload(ebmr[:], bmr_d, "ebmr")
    sload(ebmm[:], bmm_d, "ebmm")
    sload(tmpb[0:NMETA, :, :].rearrange("p a b -> p (a b)")[:, 0:128], mmm_d[:, 0:128], "tmpb")
    for l in range(2):
        sload(attb[:, l, :], ab_d[l], "attb", l)
        sload(attbv[:, l, :], abv_d[l], "attbv", l)
        sload(sinkst[32:33, l * 16:(l + 1) * 16], sink_d[l:l + 1, :], "sinkst", l)
    sload(cw[:], cw_d, "cw")
    sload(psc[:], psc_d, "psc")
    sload(pic[:], pic_d, "pic")
    tot = P.cnt["setup"]
    for b in setup_bufs:
        b.lw = ("setup", tot)

    P.op("dve", _I.memset(onesb[:], 1.0 / 1024.0), writes=[B("onesb")])
    P.op("dve", _I.memset(epsb[:], EPS), writes=[B("epsb")])
    for t_, key, md in ((ebp, "ebp", mkp_d), (ebc, "ebc", mkc_d)):
        scr_claim([B("mstage", key)])
        P.op("sp", _I.dma_start(out=mstage, in_=md), writes=[B("mstage", key)], dma=("mst", key))
        P.op("act", _I.activation(out=t_[:], in_=t_[:], func=AF.Exp), reads=[B(key)], writes=[B(key)])
        P.op("dve", _I.tensor_tensor(out=t_[:], in0=t_[:], in1=mstage, op=ALU.mult),
             reads=[B(key), B("mstage", key)], writes=[B(key)])
    for t_, key in ((ebmr, "ebmr"), (ebmm, "ebmm")):
        P.op("act", _I.activation(out=t_[:], in_=t_[:], func=AF.Exp), reads=[B(key)], writes=[B(key)])
    mm4 = ebmm[:].rearrange("p (a q) -> p a q", q=NMETA)
    msk = tmpb[0:NMETA, :, :].rearrange("p a b -> p (a b)")[:, 0:NMETA]
    msk_b = bass.AP(msk.tensor, msk.offset, [list(msk.ap[0]), [0, 16], [1, NMETA]])
    P.op("dve", _I.tensor_tensor(out=mm4, in0=mm4, in1=msk_b, op=ALU.mult),
         reads=[B("ebmm"), B("tmpb")], writes=[B("ebmm")])
    P.op("dve", _I.memset(zcar[:], 0.0), writes=[B("zcar")])
    P.op("dve", _I.memset(acar[:], 0.0), writes=[B("acar")])
    vxm3 = vxm[:].rearrange("p (a b) -> p a b", a=4)
    P.op("dve", _I.memset(vxm[:], 0.0), writes=[B("vx", "m")])
    P.op("dve", _I.memset(vxm3[32:33, :, 64:128], 1.0), writes=[B("vx", "m")])
    P.op("dve", _I.memset(vxm3[0:NMETA, :, 64:128], 1.0), writes=[B("vx", "m")])
    vx4 = vx[:].rearrange("p s (a b) -> p s a b", a=4)
    P.op("dve", _I.memset(vx[:], 1.0), writes=[B("vx", "L")] + [B("vx", s_) for s_ in range(maxpass)])
    P.op("dve", _I.memset(Pm[:], 0.0), writes=[B("Pm", kv) for kv in range(4)])
    P.op("dve", _I.memset(Pmm[:], 0.0), writes=[B("Pmm", kv) for kv in range(4)])

    xs_toggle = [0]
    scr_claim([B("stage", 0), B("stage", 1)])

    def load_tokens(src_ap, nrows, dst3, hbufs):
        i = xs_toggle[0] % 2
        xs_toggle[0] += 1
        P.op("sp", _I.dma_start(out=stage[0:nrows, i, :], in_=src_ap), writes=[B("stage", i)], dma=("xs", i))
        for half in range(2):
            bank = 2 * i + half
            for kk in range(4):
                k = half * 4 + kk
                P.op("pe", _I.transpose(
                    out=ps[bank][:, kk * 128:kk * 128 + nrows], in_=stage[0:nrows, i, k * 128:(k + 1) * 128],
                    identity=ident[0:nrows, 0:nrows]),
                    reads=[B("stage", i), B("ident")], writes=[psb[bank]], inc=(kk == 3))
            src = ps[bank][:].rearrange("p (a b) -> p a b", a=4)[:, :, 0:nrows]
            if half == 0:
                P.op("act", _I.copy(out=dst3[:, half * 4:half * 4 + 4, :], in_=src),
                     reads=[psb[bank]], writes=hbufs)
            else:
                P.op("dve", _I.tensor_copy(out=dst3[:, half * 4:half * 4 + 4, :], in_=src),
                     reads=[psb[bank]], writes=hbufs)

    load_tokens(meta_d, NMETA, hm[:], [B("hm")])
    for blk in range(NB):
        load_tokens(x_d[blk * 128:(blk + 1) * 128, :], 128, h[:, :, blk * 128:(blk + 1) * 128], [B("h", blk)])

    unit_ctr = [0]

    def load_unit(src_ap, nelem):
        u = unit_ctr[0]
        unit_ctr[0] += 1
        s = u % NSLOT
        dst = ring[:, s, 0:nelem]
        P.op("pool", _I.dma_start(out=dst, in_=src_ap), writes=[B("ring", s)], dma=("ring", s))
        return s

    class Tile:
        pass

    def scol(sl):
        return 0 if sl == "m" else (NMETA if sl == "L" else NMETA + 128 + sl * 128)

    def make_tiles(blocks, with_meta, left_block=None):
        tiles = []
        if with_meta:
            t = Tile()
            t.kind = "m"; t.c0 = 0; t.n = NMETA; t.blocks = []; t.slots = ["m"]
            t.hbufs = [B("hm")]
            t.hap = lambda k0, k1: hm[:, k0:k1, :]
            tiles.append(t)
        if left_block is not None:
            t = Tile()
            t.kind = "L"; t.c0 = NMETA; t.n = 128; t.blocks = [left_block]; t.slots = ["L"]
            t.hbufs = [B("h", left_block)]
            t.hap = lambda k0, k1, b=left_block: h[:, k0:k1, b * 128:(b + 1) * 128]
            tiles.append(t)
        pos = 0
        for nb_ in _split_tiles(len(blocks)):
            t = Tile()
            t.kind = "r"; t.c0 = NMETA + 128 + pos * 128; t.n = nb_ * 128
            t.blocks = blocks[pos:pos + nb_]; t.slots = list(range(pos, pos + nb_))
            t.hbufs = [B("h", b) for b in t.blocks]
            hc0 = t.blocks[0] * 128
            t.hap = lambda k0, k1, hc0=hc0, n=t.n: h[:, k0:k1, hc0:hc0 + n]
            tiles.append(t)
            pos += nb_
        for t in tiles:
            t.abufs = [B("abf", sl) for sl in t.slots]
            t.gb = lambda c, t=t: [B("gbf", c, sl) for sl in t.slots]
            t.ktb = [B("kt", sl) for sl in t.slots]
        return tiles

    sq_ctr = [0]
    nrm_ctr = [0]

    def rms_stats(hsrc, hb, n):
        ri = nrm_ctr[0] % 2
        nrm_ctr[0] += 1
        bank = 6 + ri
        for k in range(KC):
            si = sq_ctr[0] % 2
            sq_ctr[0] += 1
            P.op("act", _I.activation(out=sq[:, si, 0:n], in_=hsrc(k), func=AF.Square),
                 reads=hb, writes=[B("sq", si)])
            P.op("pe", _I.matmul(ps[bank][:, 0:n], lhsT=onesb[:], rhs=sq[:, si, 0:n],
                                                      start=(k == 0), stop=(k == KC - 1)),
                 reads=[B("sq", si), B("onesb")], writes=[psb[bank]], inc=True)
        P.op("act", _I.activation(out=rstd[:, ri, 0:n], in_=ps[bank][:, 0:n], func=AF.Ln, bias=epsb[:], scale=1.0),
             reads=[psb[bank], B("epsb")], writes=[B("rstd", ri)])
        P.op("act", _I.activation(out=rstd[:, ri, 0:n], in_=rstd[:, ri, 0:n], func=AF.Exp, scale=-0.5),
             reads=[B("rstd", ri)], writes=[B("rstd", ri)])
        return ri

    def tile_hsrc(t):
        return (lambda k: t.hap(k, k + 1)[:, 0, :]), t.hbufs, t.n

    def norm_to_abf(t, gcolap):
        hsrc, hb, n = tile_hsrc(t)
        ri = rms_stats(hsrc, hb, n)
        for k in range(KC):
            P.op("dve", _I.scalar_tensor_tensor(out=abf[:, k, t.c0:t.c0 + n], in0=hsrc(k), scalar=gcolap[:, k:k + 1],
                                                              in1=rstd[:, ri, 0:n], op0=ALU.mult, op1=ALU.mult),
                 reads=hb + [B("rstd", ri), B("gam")], writes=t.abufs)

    bank_ctr = [0]

    def next_bank(lo, hi):
        b = lo + bank_ctr[0] % (hi - lo)
        bank_ctr[0] += 1
        return b

    def h_add(t, m, bank, scalar_ap=None, mult=False):
        dst = t.hap(m, m + 1)[:, 0, :]
        if scalar_ap is None:
            P.op("dve", _I.tensor_tensor(out=dst, in0=ps[bank][:, 0:t.n], in1=dst, op=ALU.add),
                 reads=[psb[bank]] + t.hbufs, writes=t.hbufs)
        else:
            P.op("dve", _I.scalar_tensor_tensor(out=dst, in0=ps[bank][:, 0:t.n], scalar=scalar_ap, in1=dst,
                                                         op0=(ALU.mult if mult else ALU.add), op1=ALU.add),
                 reads=[psb[bank]] + t.hbufs, writes=t.hbufs)

    def proj(s, w_off, tiles, src3, src_bufs_fn, consume, nk=KC, wstride=128):
        for t in tiles:
            bank = next_bank(0, 6)
            for k in range(nk):
                P.op("pe", _I.matmul(
                    ps[bank][:, 0:t.n], lhsT=ring[:, s, w_off + k * wstride:w_off + k * wstride + 128],
                    rhs=src3[:, k, t.c0:t.c0 + t.n], start=(k == 0), stop=(k == nk - 1)),
                    reads=[B("ring", s)] + src_bufs_fn(t, k), writes=[psb[bank]], inc=(k == nk - 1))
            consume(t, bank)

    FGRP = ((0, 8), (8, 7), (15, 7))

    def ffn_pass(l, tiles, hoist=None):
        P.alias(ET_ALL, ABF_ALL)
        for t in tiles:
            norm_to_abf(t, gcol["ffn"](l))
        for g, (j0, nj) in enumerate(FGRP):
            for jj in range(nj):
                s = load_unit(gu_d[l, j0 + jj], 2 * KC * 128)
                for t in tiles:
                    bg = next_bank(0, 6)
                    bu = next_bank(0, 6)
                    for which, bank in ((0, bg), (1, bu)):
                        for k in range(KC):
                            off = which * KC * 128 + k * 128
                            P.op("pe", _I.matmul(
                                ps[bank][:, 0:t.n], lhsT=ring[:, s, off:off + 128], rhs=abf[:, k, t.c0:t.c0 + t.n],
                                start=(k == 0), stop=(k == KC - 1)),
                                reads=[B("ring", s)] + t.abufs, writes=[psb[bank]], inc=(k == KC - 1))
                    si = next_bank(0, 2)
                    P.op("act", _I.activation(out=sgt[:, si, 0:t.n], in_=ps[bg][:, 0:t.n], func=AF.Silu),
                         reads=[psb[bg]], writes=[B("sgt", si)])
                    P.op("dve", _I.tensor_tensor(
                        out=gbf[:, jj, t.c0:t.c0 + t.n], in0=ps[bu][:, 0:t.n], in1=sgt[:, si, 0:t.n], op=ALU.mult),
                        reads=[psb[bu], B("sgt", si)], writes=t.gb(jj))
            if g == len(FGRP) - 1 and hoist is not None:
                hoist()
            for m in range(KC):
                s = load_unit(dn_d[l, g, m][:, 0:nj * 128], nj * 128)
                for t in tiles:
                    bank = next_bank(0, 6)
                    for jj in range(nj):
                        P.op("pe", _I.matmul(
                            ps[bank][:, 0:t.n], lhsT=ring[:, s, jj * 128:(jj + 1) * 128], rhs=gbf[:, jj, t.c0:t.c0 + t.n],
                            start=(jj == 0), stop=(jj == nj - 1)),
                            reads=[B("ring", s)] + t.gb(jj), writes=[psb[bank]], inc=(jj == nj - 1))
                    h_add(t, m, bank)

    att_ctr = [0]
    ones_ok = {}

    def vslot(sl):
        return 0 if sl == "L" else sl + 1

    def qbufs(kv, sl):
        return [B("gbf", 2 * kv, sl), B("gbf", 2 * kv + 1, sl)]

    def bcast_q(ap2, nq):
        return bass.AP(ap2.tensor, ap2.offset, [list(ap2.ap[0]), list(ap2.ap[1]), [0, nq]])

    def att_A(qsl, qn, kv, chunks, kind):
        ai = att_ctr[0] % 2
        att_ctr[0] += 1
        qc0 = scol(qsl)
        W2 = 2 * qn
        W4 = 4 * qn
        sbanks = []
        for ci, ch in enumerate(chunks):
            nk = ch["nk"]
            kc0 = scol(ch["kt_sl"])
            bank = (ci if kind == "r" else 2) + 3 * ai
            sbanks.append(bank)
            for hi in range(2):
                P.op("pe", _I.matmul(
                    ps[bank][0:nk, hi * W2:(hi + 1) * W2].rearrange("p (a b) -> p a b", a=2),
                    lhsT=kt[:, 2 * kv + hi, kc0:kc0 + nk], rhs=qt[:, 2 * kv:2 * kv + 2, qc0:qc0 + qn], start=True, stop=True),
                    reads=[B("kt", ch["kt_sl"])] + qbufs(kv, qsl), writes=[psb[bank]], inc=(hi == 1))
        pv = []
        for ci, ch in enumerate(chunks):
            nk = ch["nk"]
            bank = sbanks[ci]
            if ch["band"]:
                e_ap = Et[:, ai, ci, 0:W4]; e_buf = B("Et", ai, ci)
                p_ap = Pt[:, ai, ci, 0:W4]; p_buf = B("Pt", ai, ci)
                full_p = p_ap
            elif kind == "r":
                e_ap = Em[0:NMETA, ai, 0:W4]; e_buf = B("Em", ai)
                p_ap = Pm[0:NMETA, kv, 0:W4]; p_buf = B("Pm", kv)
                full_p = Pm[0:33, kv, 0:W4]
            else:
                e_ap = Em[0:NMETA, ai, 0:W4]; e_buf = B("Em", ai)
                p_ap = Pmm[0:NMETA, kv, 0:W4]; p_buf = B("Pmm", kv)
                full_p = Pmm[0:33, kv, 0:W4]
            P.op("act", _I.activation(out=e_ap, in_=ps[bank][0:nk, 0:W4], func=AF.Exp, scale=0.125),
                 reads=[psb[bank]], writes=[e_buf])
            if ch.get("ebq"):
                in0 = e_ap.rearrange("p (s q) -> p s q", s=4)
                outp = p_ap.rearrange("p (s q) -> p s q", s=4)
                P.op("dve", _I.tensor_tensor(out=outp, in0=in0, in1=ch["eb_ap"], op=ALU.mult),
                     reads=[e_buf, B(ch["eb_key"])], writes=[p_buf])
            else:
                P.op("dve", _I.tensor_tensor(out=p_ap, in0=e_ap, in1=ch["eb_ap"], op=ALU.mult),
                     reads=[e_buf, B(ch["eb_key"])], writes=[p_buf])
            pv.append((ch["vx_ap"], ch["vx_buf"], full_p, p_buf))
        return dict(ai=ai, pv=pv, qc0=qc0, qn=qn, kv=kv, qsl=qsl, W2=W2, W4=W4)

    def att_B(c):
        ai, pv, qc0, qn, kv, qsl, W2, W4 = c["ai"], c["pv"], c["qc0"], c["qn"], c["kv"], c["qsl"], c["W2"], c["W4"]
        obank = 6 + ai
        for ci, (vx_ap, vx_buf, full_p, p_buf) in enumerate(pv):
            P.op("pe", _I.matmul(
                ps[obank][:, 0:W4], lhsT=vx_ap, rhs=full_p, start=(ci == 0), stop=(ci == len(pv) - 1)),
                reads=[vx_buf, p_buf], writes=[psb[obank]], inc=(ci == len(pv) - 1))
        P.op("act", _I.activation(out=rden[64:128, ai, 0:W4], in_=ps[obank][64:128, 0:W4], func=AF.Ln),
             reads=[psb[obank]], writes=[B("sgt", ai)])
        P.op("act", _I.activation(out=rden[64:128, ai, 0:W4], in_=rden[64:128, ai, 0:W4], func=AF.Exp, scale=-1.0),
             reads=[B("sgt", ai)], writes=[B("sgt", ai)])
        for hi in range(2):
            o_out = qt[hi * 64:(hi + 1) * 64, 2 * kv:2 * kv + 2, qc0:qc0 + qn]
            o_in = ps[obank][0:64, hi * W2:(hi + 1) * W2].rearrange("p (a b) -> p a b", a=2)
            r_in = rden[64:128, ai, hi * W2:(hi + 1) * W2].rearrange("p (a b) -> p a b", a=2)
            P.op("dve", _I.tensor_tensor(out=o_out, in0=o_in, in1=r_in, op=ALU.mult),
                 reads=[psb[obank], B("sgt", ai)], writes=qbufs(kv, qsl))

    def mixer_pre(l, pi, blocks, with_left):
        first = (pi == 0)
        left_block = blocks[0] - 1
        tiles = make_tiles(blocks, with_meta=first, left_block=(left_block if (first and with_left) else None))
        P.alias(ET_ALL, ABF_ALL)
        for t in tiles:
            norm_to_abf(t, gcol["mix"](l))
        return tiles

    def attn_pass(l, j, pi, blocks, tiles):
        first = (pi == 0)
        rtiles = [t for t in tiles if t.kind == "r"]
        qtiles = [t for t in tiles if t.kind != "L"]
        if first:
            scr_claim([B("ebmf")])
            P.op("sp", _I.dma_start(out=ebmf, in_=bmf_d), writes=[B("ebmf")], dma=("ebmf", l))
            P.op("act", _I.activation(out=ebmf, in_=ebmf, func=AF.Exp), reads=[B("ebmf")], writes=[B("ebmf")])
            srow = sinkst[32:33, j * 16:(j + 1) * 16]
            P.op("act", _I.activation(out=Pm[32:33, :, :].rearrange("p a (s q) -> p (a s) q", s=4),
                                               in_=bcast_q(srow, 128), func=AF.Exp),
                 reads=[B("sinkst", j)], writes=[B("Pm", kv) for kv in range(4)])
            P.op("act", _I.activation(out=Pmm[32:33, :, :].rearrange("p a (s q) -> p (a s) q", s=4),
                                               in_=bcast_q(srow, NMETA), func=AF.Exp),
                 reads=[B("sinkst", j)], writes=[B("Pmm", kv) for kv in range(4)])
        abf_src = lambda t, k: t.abufs
        for c in range(8):
            s = load_unit(aq_d[j, c], KC * 128)

            def cons(t, bank, c=c):
                P.op("act", _I.activation(out=qt[:, c, t.c0:t.c0 + t.n], in_=ps[bank][:, 0:t.n], func=AF.Identity,
                                                   bias=attb[:, j, c:c + 1], scale=1.0),
                     reads=[psb[bank], B("attb", j)], writes=t.gb(c))
            proj(s, 0, qtiles, abf, abf_src, cons)
        for c in range(8):
            s = load_unit(ak_d[j, c], KC * 128)

            def cons(t, bank, c=c):
                P.op("act", _I.activation(out=kt[:, c, t.c0:t.c0 + t.n], in_=ps[bank][:, 0:t.n], func=AF.Identity,
                                                   bias=attb[:, j, 8 + c:9 + c], scale=1.0),
                     reads=[psb[bank], B("attb", j)], writes=t.ktb)
            proj(s, 0, tiles, abf, abf_src, cons)
        s = load_unit(av_d[j], KC * 256)
        for t in tiles:
            for sbi, sl in enumerate(t.slots):
                if t.kind == "m":
                    rows = NMETA; dst = vxm3[0:NMETA, :, 0:64]; wb = [B("vx", "m")]; gbk = None
                else:
                    rows = 128
                    gbk = t.blocks[sbi]
                    dst = vx4[:, vslot(sl), :, 0:64]
                    wb = [B("vx", sl)]
                    if not ones_ok.get(vslot(sl), True):
                        P.op("dve", _I.memset(vx4[:, vslot(sl), :, 64:128], 1.0), writes=wb)
                        ones_ok[vslot(sl)] = True
                cc0 = scol(sl)
                bank = next_bank(0, 6)
                for k in range(KC):
                    P.op("pe", _I.matmul(
                        ps[bank][0:rows, 0:256], lhsT=abf[:, k, cc0:cc0 + rows], rhs=ring[:, s, k * 256:(k + 1) * 256],
                        start=(k == 0), stop=(k == KC - 1)),
                        reads=[B("ring", s), B("abf", sl)], writes=[psb[bank]], inc=(k == KC - 1))
                P.op("dve", _I.tensor_tensor(
                    out=dst, in0=ps[bank][0:rows, 0:256].rearrange("p (a b) -> p a b", a=4),
                    in1=attbv[0:rows, j, :].rearrange("p (a b) -> p a b", a=4), op=ALU.add),
                    reads=[psb[bank], B("attbv", j)], writes=wb)
                if gbk == NHALO - 1:
                    vs = vslot(sl)
                    P.op("dve", _I.tensor_scalar_mul(out=vx[:, vs, :], in0=vx[:, vs, :], scalar1=flags[:, 1:2]),
                         reads=wb + [B("flags")], writes=wb)
                    ones_ok[vs] = False
        P.alias(ABF_ALL, ET_ALL)

        def meta_chunk(kv, first_blk):
            d = dict(kt_sl="m", nk=NMETA, vx_ap=vxm[0:33, kv * 128:(kv + 1) * 128], vx_buf=B("vx", "m"), band=False)
            if first_blk == "mm":
                d.update(eb_ap=ebmm[:, kv * 64:(kv + 1) * 64], eb_key="ebmm")
            elif first_blk:
                d.update(eb_ap=ebmf[:, kv * 512:(kv + 1) * 512], eb_key="ebmf")
            else:
                d.update(eb_ap=bcast_q(ebmr[:, kv * 4:(kv + 1) * 4], 128), eb_key="ebmr", ebq=True)
            return d
        items = []
        if first:
            for kv in range(4):
                items.append(("m", NMETA, kv, [meta_chunk(kv, "mm")], "m"))
        for t in rtiles:
            for sbi, sl in enumerate(t.slots):
                gbk = t.blocks[sbi]
                psl = "L" if sl == 0 else sl - 1
                for kv in range(4):
                    chunks = [
                        dict(kt_sl=psl, nk=128, vx_ap=vx[:, vslot(psl), kv * 128:(kv + 1) * 128], vx_buf=B("vx", psl),
                             eb_ap=ebp[:, kv * 512:(kv + 1) * 512], eb_key="ebp", band=True),
                        dict(kt_sl=sl, nk=128, vx_ap=vx[:, vslot(sl), kv * 128:(kv + 1) * 128], vx_buf=B("vx", sl),
                             eb_ap=ebc[:, kv * 512:(kv + 1) * 512], eb_key="ebc", band=True),
                        meta_chunk(kv, gbk == NHALO),
                    ]
                    items.append((sl, 128, kv, chunks, "r"))
        pending = None
        for it in items:
            ctx = att_A(*it)
            if pending is not None:
                att_B(pending)
            pending = ctx
        if pending is not None:
            att_B(pending)
        for m in range(8):
            s = load_unit(ao_d[j, m], KC * 128)

            def cons(t, bank, m=m):
                h_add(t, m, bank, scalar_ap=attb[:, j, 16 + m:17 + m])
            proj(s, 0, qtiles, qt, lambda t, k: t.gb(k), cons)
        return tiles

    def attn_carry(blocks):
        lsl = len(blocks) - 1
        lc0 = scol(lsl)
        P.op("dve", _I.tensor_copy(out=kt[:, :, NMETA:NMETA + 128], in_=kt[:, :, lc0:lc0 + 128]),
             reads=[B("kt", lsl)], writes=[B("kt", "L")])
        P.op("dve", _I.tensor_copy(out=vx[:, 0, :], in_=vx[:, vslot(lsl), :]),
             reads=[B("vx", lsl)], writes=[B("vx", "L")])
        ones_ok[0] = ones_ok.get(vslot(lsl), True)

    def conv_pass(l, pi, blocks, tiles):
        first = (pi == 0)
        if first:
            scr_claim([B("zt", 0), B("zt", 1), B("cacc", 0), B("cacc", 1)])
        ybf = gbf
        for c in range(8):
            slots3 = [load_unit(ci_d[c, part], KC * 128) for part in range(3)]
            for t in tiles:
                banks = [next_bank(0, 8) for _ in range(3)]
                for part in range(3):
                    for k in range(KC):
                        P.op("pe", _I.matmul(
                            ps[banks[part]][:, 0:t.n], lhsT=ring[:, slots3[part], k * 128:(k + 1) * 128], rhs=abf[:, k, t.c0:t.c0 + t.n],
                            start=(k == 0), stop=(k == KC - 1)),
                            reads=[B("ring", slots3[part])] + t.abufs, writes=[psb[banks[part]]], inc=(k == KC - 1))
                bB, bC, bU = banks
                zi = next_bank(0, 2)
                n = t.n
                P.op("act", _I.copy(out=sgt[:, zi, 0:n], in_=ps[bU][:, 0:n]),
                     reads=[psb[bU]], writes=[B("sgt", zi)])
                if t.kind == "m":
                    P.op("dve", _I.memset(zt[:, zi, 0:2], 0.0), writes=[B("zt", zi)])
                else:
                    P.op("dve", _I.tensor_copy(out=zt[:, zi, 0:2], in_=zcar[:, c, :]),
                         reads=[B("zcar")], writes=[B("zt", zi)])
                P.op("dve", _I.tensor_tensor(out=zt[:, zi, 2:2 + n], in0=ps[bC][:, 0:n], in1=sgt[:, zi, 0:n], op=ALU.mult),
                     reads=[psb[bC], B("sgt", zi)], writes=[B("zt", zi)])
                if t.kind != "m":
                    P.op("dve", _I.tensor_copy(out=zcar[:, c, :], in_=zt[:, zi, n:n + 2]),
                         reads=[B("zt", zi)], writes=[B("zcar")])
                P.op("dve", _I.tensor_scalar_mul(out=cacc[:, zi, 0:n], in0=zt[:, zi, 0:n], scalar1=cw[:, c:c + 1]),
                     reads=[B("zt", zi), B("cw")], writes=[B("cacc", zi)])
                for tap in (1, 2):
                    P.op("dve", _I.scalar_tensor_tensor(
                        out=cacc[:, zi, 0:n], in0=zt[:, zi, tap:tap + n], scalar=cw[:, tap * KC + c:tap * KC + c + 1],
                        in1=cacc[:, zi, 0:n], op0=ALU.mult, op1=ALU.add),
                        reads=[B("zt", zi), B("cw"), B("cacc", zi)], writes=[B("cacc", zi)])
                P.op("dve", _I.tensor_tensor(out=ybf[:, c, t.c0:t.c0 + n], in0=ps[bB][:, 0:n],
                                                                                  in1=cacc[:, zi, 0:n], op=ALU.mult),
                     reads=[psb[bB], B("cacc", zi)], writes=t.gb(c))
        for m in range(8):
            s = load_unit(co_d[m], KC * 128)

            def cons(t, bank, m=m):
                h_add(t, m, bank)
            proj(s, 0, tiles, ybf, lambda t, k: t.gb(k), cons)
        return tiles

    def pool_pass(l, pi, blocks):
        first = (pi == 0)
        tiles = make_tiles(blocks, with_meta=first)
        mixbf = gbf
        if first:
            scr_claim([B("afk", 0), B("afk", 1)] + [B("pls", i) for i in range(3)])
        s = load_unit(pw_d, 4 * 2 * 256)
        seen_real = False
        for t in tiles:
            n = t.n
            if t.kind == "m":
                hsrc, hb, nn = tile_hsrc(t)
                mode = "zero"; off = NMETA
            elif first and not seen_real:
                hc0 = t.blocks[0] * 128 - NMETA
                nn = n + NMETA
                hsrc = lambda k, hc0=hc0, nn=nn: h[:, k, hc0:hc0 + nn]
                hb = t.hbufs + [B("h", t.blocks[0] - 1)]
                mode = "own"; off = 0
            else:
                hsrc, hb, nn = tile_hsrc(t)
                mode = "carry"; off = NMETA
            ri = rms_stats(hsrc, hb, nn)
            W = NMETA + n
            for k in range(KC):
                ki = k % 2
                if mode == "zero":
                    P.op("dve", _I.memset(afk[:, ki, 0:NMETA], 0.0), writes=[B("afk", ki)])
                elif mode == "carry":
                    P.op("dve", _I.tensor_copy(out=afk[:, ki, 0:NMETA], in_=acar[:, k, :]),
                         reads=[B("acar")], writes=[B("afk", ki)])
                P.op("dve", _I.scalar_tensor_tensor(
                    out=afk[:, ki, off:off + nn], in0=hsrc(k), scalar=gcol["mix"](l)[:, k:k + 1], in1=rstd[:, ri, 0:nn],
                    op0=ALU.mult, op1=ALU.mult),
                    reads=hb + [B("rstd", ri), B("gam")], writes=[B("afk", ki)])
                if t.kind != "m":
                    P.op("dve", _I.tensor_copy(out=acar[:, k, :], in_=afk[:, ki, n:n + NMETA]),
                         reads=[B("afk", ki)], writes=[B("acar")])
                w = POOL_W[k]
                step = 1
                si = 0
                while step < w:
                    dsti = si % 3
                    lo = 2 * step - 1
                    if step == 1:
                        in0 = afk[:, ki, lo:W]; in1 = afk[:, ki, lo - step:W - step]; rb = [B("afk", ki)]
                    else:
                        prev = (si - 1) % 3
                        in0 = pls[:, prev, lo:W]; in1 = pls[:, prev, lo - step:W - step]; rb = [B("pls", prev)]
                    P.op("dve", _I.tensor_tensor(out=pls[:, dsti, lo:W], in0=in0, in1=in1, op=ALU.add),
                         reads=rb, writes=[B("pls", dsti)])
                    step *= 2
                    si += 1
                last = (si - 1) % 3
                if t.kind == "m":
                    P.op("dve", _I.tensor_tensor(out=pls[:, last, NMETA:W], in0=pls[:, last, NMETA:W],
                                                                           in1=pic[:, k * NMETA:(k + 1) * NMETA], op=ALU.mult),
                         reads=[B("pls", last), B("pic")], writes=[B("pls", last)])
                    P.op("dve", _I.tensor_tensor(out=mixbf[:, k, t.c0:t.c0 + n], in0=pls[:, last, NMETA:W],
                                                                                       in1=afk[:, ki, NMETA:W], op=ALU.subtract),
                         reads=[B("pls", last), B("afk", ki)], writes=t.gb(k))
                else:
                    P.op("dve", _I.scalar_tensor_tensor(
                        out=mixbf[:, k, t.c0:t.c0 + n], in0=pls[:, last, NMETA:W], scalar=1.0 / w, in1=afk[:, ki, NMETA:W],
                        op0=ALU.mult, op1=ALU.subtract),
                        reads=[B("pls", last), B("afk", ki)], writes=t.gb(k))
            if t.kind != "m":
                seen_real = True
            for g in range(4):
                for dc in range(2):
                    m = 2 * g + dc
                    bank = next_bank(0, 6)
                    for cc in range(2):
                        off2 = (g * 2 + cc) * 256 + dc * 128
                        P.op("pe", _I.matmul(
                            ps[bank][:, 0:t.n], lhsT=ring[:, s, off2:off2 + 128], rhs=mixbf[:, 2 * g + cc, t.c0:t.c0 + t.n],
                            start=(cc == 0), stop=(cc == 1)),
                            reads=[B("ring", s)] + t.gb(2 * g + cc), writes=[psb[bank]], inc=(cc == 1))
                    h_add(t, m, bank, scalar_ap=psc[:, m:m + 1], mult=True)
        return tiles

    def blend():
        c0 = NHALO * 128 - NMETA
        P.op("dve", _I.tensor_scalar_mul(out=tmpb[:], in0=hm[:], scalar1=flags[:, 0:1]),
             reads=[B("hm"), B("flags")], writes=[B("tmpb")])
        P.op("dve", _I.scalar_tensor_tensor(out=h[:, :, c0:c0 + NMETA], in0=h[:, :, c0:c0 + NMETA], scalar=flags[:, 1:2],
                                                     in1=tmpb[:], op0=ALU.mult, op1=ALU.add),
             reads=[B("tmpb"), B("h", NHALO - 1), B("flags")], writes=[B("h", NHALO - 1)])

    q0s = [1, 1, 2, 3]
    steps = []
    for l in range(n_layers):
        q0 = min(q0s[l], NB - 1)
        for pi, blocks in enumerate(_passes(q0, NB)):
            steps.append((l, pi, blocks))

    def step_pre(si):
        l, pi, blocks = steps[si]
        kind = l % 3
        if pi == 0 and kind in (1, 2):
            blend()
        if kind == 2:
            return None
        return mixer_pre(l, pi, blocks, with_left=(kind == 0))

    pre_tiles = step_pre(0) if steps else None
    for si, (l, pi, blocks) in enumerate(steps):
        kind, j = l % 3, l // 3
        npass = sum(1 for st in steps if st[0] == l)
        if kind == 0:
            tiles = attn_pass(l, j, pi, blocks, pre_tiles)
            if pi + 1 < npass:
                attn_carry(blocks)
        elif kind == 1:
            tiles = conv_pass(l, pi, blocks, pre_tiles)
        else:
            tiles = pool_pass(l, pi, blocks)
        nxt = {}

        def hoist(si=si, nxt=nxt):
            nxt["tiles"] = step_pre(si + 1)
        can_hoist = False
        if si + 1 < len(steps):
            nl_, npi, nblocks = steps[si + 1]
            touched = set(nblocks) | {nblocks[0] - 1, nblocks[0] - 2}
            can_hoist = not (touched & set(blocks)) and not (npi == 0 and pi == 0)
        ffn_pass(l, [t for t in tiles if t.kind != "L"], hoist if can_hoist else None)
        if si + 1 < len(steps) and not can_hoist:
            hoist()
        pre_tiles = nxt.get("tiles")

    out_ctr = [0]
    scr_claim([B("stage", 0), B("stage", 1)])
    P.alias(ABF_ALL + ET_ALL, [B("yf", 0), B("yf", 1)])

    def emit_rows(src_fn, src_bufs, nrows, row0):
        oi = out_ctr[0] % 2
        out_ctr[0] += 1
        for half in range(2):
            bank = 2 * oi + half
            for kk in range(4):
                k = half * 4 + kk
                P.op("pe", _I.transpose(
                    out=ps[bank][0:nrows, kk * 128:(kk + 1) * 128], in_=src_fn(k), identity=ident[:]),
                    reads=src_bufs + [B("ident")], writes=[psb[bank]], inc=(kk == 3))
            if half == 0:
                P.op("act", _I.copy(out=stage[0:nrows, oi, 0:512], in_=ps[bank][0:nrows, :]),
                     reads=[psb[bank]], writes=[B("stage", oi)])
            else:
                P.op("dve", _I.tensor_copy(out=stage[0:nrows, oi, 512:1024], in_=ps[bank][0:nrows, :]),
                     reads=[psb[bank]], writes=[B("stage", oi)])
        P.op("sp", _I.dma_start(out=out_d[row0:row0 + nrows, :], in_=stage[0:nrows, oi, :]),
             reads=[B("stage", oi)], dma=("outq", oi))

    if debug_h:
        for bi in range(n_own):
            blk = NHALO + bi
            emit_rows(lambda k, blk=blk: h[:, k, blk * 128:(blk + 1) * 128], [B("h", blk)], 128, bi * 128)
        emit_rows(lambda k: hm[:, k, :], [B("hm")], NMETA, n_own * 128)
    else:
        def fin_norm(bi):
            blk = NHALO + bi
            hsrc = lambda k, blk=blk: h[:, k, blk * 128:(blk + 1) * 128]
            hb = [B("h", blk)]
            ri = rms_stats(hsrc, hb, 128)
            yi = bi % 2
            for k in range(KC):
                P.op("dve", _I.scalar_tensor_tensor(
                    out=yfv[yi][:, k, :], in0=hsrc(k), scalar=gcol["fin"](0)[:, k:k + 1], in1=rstd[:, ri, 0:128],
                    op0=ALU.mult, op1=ALU.mult),
                    reads=hb + [B("rstd", ri), B("gam")], writes=[B("yf", yi)])

        def fin_out(bi):
            yi = bi % 2
            emit_rows(lambda k, yi=yi: yfv[yi][:, k, :], [B("yf", yi)], 128, bi * 128)

        for bi in range(n_own):
            fin_norm(bi)
            if bi > 0:
                fin_out(bi - 1)
        fin_out(n_own - 1)

    P.emit(nc, final_waits=tuple(k for k in (("outq", 0), ("outq", 1)) if k in P.cnt))
    return nc


def _prep_shared(inp):
    f = lambda a: np.ascontiguousarray(np.asarray(a, dtype=np.float32))
    sh = {}
    sh["meta"] = f(inp["meta_tokens"])
    sh["ident"] = np.eye(128, dtype=np.float32)
    gam = np.concatenate([_fm(f(inp["norm_mix"])).reshape(128, 4 * KC), _fm(f(inp["norm_ffn"])).reshape(128, 4 * KC),
                          _fm(f(inp["norm_final"])).reshape(128, KC)], axis=1)
    sh["gammas"] = np.ascontiguousarray(gam)
    wg, wu, wd = f(inp["ffn_w_gate"]), f(inp["ffn_w_up"]), f(inp["ffn_w_down"])
    gu = np.empty((4, FC, 128, 2, KC, 128), np.float32)
    dn = np.zeros((4, 3, KC, 128, 8, 128), np.float32)
    for l in range(4):
        gu[l, :, :, 0] = _arr(wg[l])
        gu[l, :, :, 1] = _arr(wu[l])
        a = _arr(wd[l])
        dn[l, 0, :, :, 0:8] = a[:, :, 0:8, :]
        dn[l, 1, :, :, 0:7] = a[:, :, 8:15, :]
        dn[l, 2, :, :, 0:7] = a[:, :, 15:22, :]
    sh["ffn_gu"] = gu.reshape(4, FC, 128, 2 * KC * 128)
    sh["ffn_dn"] = dn.reshape(4, 3, KC, 128, 8 * 128)
    wqkv, bqkv = f(inp["attn_w_qkv"]), f(inp["attn_b_qkv"])
    wo, bo = f(inp["attn_w_o"]), f(inp["attn_b_o"])
    aq = np.empty((2, 8, 128, KC, 128), np.float32)
    ak = np.zeros((2, 8, 128, KC, 128), np.float32)
    av = np.empty((2, 128, KC, 256), np.float32)
    ao = np.empty((2, 8, 128, KC, 128), np.float32)
    ab = np.zeros((2, 128, 24), np.float32)
    abv = np.empty((2, 128, 256), np.float32)
    for j in range(2):
        aq[j] = _arr(wqkv[j][:, 0:1024])
        wk = wqkv[j][:, 1024:1280]
        bk = bqkv[j][1024:1280]
        for kv in range(4):
            wkk = wk[:, kv * 64:(kv + 1) * 64].reshape(KC, 128, 64).transpose(1, 0, 2)
            ak[j, 2 * kv, :, :, 0:64] = wkk
            ak[j, 2 * kv + 1, :, :, 64:128] = wkk
            ab[j, 0:64, 8 + 2 * kv] = bk[kv * 64:(kv + 1) * 64]
            ab[j, 64:128, 8 + 2 * kv + 1] = bk[kv * 64:(kv + 1) * 64]
        av[j] = wqkv[j][:, 1280:1536].reshape(KC, 128, 256).transpose(1, 0, 2)
        ao[j] = _arr(wo[j])
        ab[j, :, 0:8] = _fm(bqkv[j][0:1024])
        ab[j, :, 16:24] = _fm(bo[j])
        abv[j] = np.broadcast_to(bqkv[j][1280:1536], (128, 256))
    sh["attn_q"] = aq.reshape(2, 8, 128, KC * 128)
    sh["attn_k"] = ak.reshape(2, 8, 128, KC * 128)
    sh["attn_v"] = av.reshape(2, 128, KC * 256)
    sh["attn_o"] = ao.reshape(2, 8, 128, KC * 128)
    sh["attn_b"] = ab
    sh["attn_bv"] = abv
    sinks = f(inp["attn_sinks"])
    sk = np.empty((2, 4, 4), np.float32)
    for j in range(2):
        for kv in range(4):
            for s_, g in enumerate(SLOT_ORDER):
                sk[j, kv, s_] = sinks[j, 4 * kv + g]
    sh["attn_sink"] = sk.reshape(2, 16)
    tab = f(inp["rel_bias_table"])
    bt = _bucket_table()
    kk = np.arange(128)[:, None]
    qq = np.arange(128)[None, :]
    d_prev = 128 + qq - kk
    d_cur = qq - kk
    m_prev = ((d_prev >= 0) & (d_prev < 128)).astype(np.float32)
    m_cur = ((d_cur >= 0) & (d_cur < 128)).astype(np.float32)
    bp = np.empty((128, 4, 4, 128), np.float32)
    bc = np.empty((128, 4, 4, 128), np.float32)
    for kv in range(4):
        for s_, g in enumerate(SLOT_ORDER):
            hh = 4 * kv + g
            bp[:, kv, s_, :] = tab[bt[np.clip(d_prev, 0, 511)], hh]
            bc[:, kv, s_, :] = tab[bt[np.clip(d_cur, 0, 511)], hh]
    sh["bias_prev"] = bp.reshape(128, 2048)
    sh["bias_cur"] = bc.reshape(128, 2048)
    sh["mask_prev"] = np.ascontiguousarray(np.broadcast_to(m_prev[:, None, None, :], (128, 4, 4, 128))).reshape(128, 2048)
    sh["mask_cur"] = np.ascontiguousarray(np.broadcast_to(m_cur[:, None, None, :], (128, 4, 4, 128))).reshape(128, 2048)
    mm_ = np.arange(NMETA)[:, None]
    d_first = NMETA + qq - mm_
    bmf = np.empty((NMETA, 4, 4, 128), np.float32)
    bmr = np.empty((NMETA, 4, 4), np.float32)
    mq = np.arange(NMETA)[None, :]
    d_mm = mq - mm_
    bmm = np.empty((NMETA, 4, 4, NMETA), np.float32)
    for kv in range(4):
        for s_, g in enumerate(SLOT_ORDER):
            hh = 4 * kv + g
            bmf[:, kv, s_, :] = tab[bt[d_first], hh]
            bmr[:, kv, s_] = tab[31, hh]
            bmm[:, kv, s_, :] = tab[bt[np.clip(d_mm, 0, 511)], hh]
    sh["_bmf"] = bmf.reshape(NMETA, 2048)
    sh["bias_meta_rest"] = bmr.reshape(NMETA, 16)
    sh["_bmr_full"] = np.ascontiguousarray(np.broadcast_to(bmr[:, :, :, None], (NMETA, 4, 4, 128))).reshape(NMETA, 2048)
    sh["bias_mm"] = bmm.reshape(NMETA, 256)
    sh["mask_mm"] = np.ascontiguousarray(np.broadcast_to((d_mm >= 0).astype(np.float32)[:, None, None, :], (NMETA, 4, 4, NMETA))).reshape(NMETA, 256)
    cin = f(inp["conv_w_in"])[0]
    ci = np.empty((8, 3, 128, KC, 128), np.float32)
    for part in range(3):
        ci[:, part] = _arr(cin[:, part * 1024:(part + 1) * 1024])
    sh["conv_in"] = ci.reshape(8, 3, 128, KC * 128)
    sh["conv_out"] = _arr(f(inp["conv_w_out"])[0]).reshape(8, 128, KC * 128)
    sh["conv_w"] = _fm(f(inp["conv_w"])[0]).reshape(128, 3 * KC)
    pw = f(inp["pool_w"])[0]
    sh["pool_w"] = np.ascontiguousarray(pw.reshape(4, 2, 128, 256).transpose(2, 0, 1, 3)).reshape(128, 2048)
    sh["pool_scale"] = _fm(f(inp["pool_scale"])[0]).reshape(128, KC)
    invc = np.empty((128, KC, NMETA), np.float32)
    tt = np.arange(NMETA)
    for k in range(KC):
        invc[:, k, :] = (1.0 / np.minimum(POOL_W[k], tt + 1)).astype(np.float32)[None, :]
    sh["pool_invc"] = invc.reshape(128, KC * NMETA)
    return sh


def _core_inputs(x, sh, c, n_own):
    b, ch = divmod(c, SEQ // CHUNK)
    s0 = ch * CHUNK
    T = (NHALO + n_own) * 128
    xc = np.zeros((T, D), np.float32)
    lo = s0 - NHALO * 128
    src_lo = max(lo, 0)
    hi = s0 + n_own * 128
    xc[src_lo - lo:, :] = x[b, src_lo:hi, :]
    m = {k: v for k, v in sh.items() if not k.startswith("_")}
    m["x"] = xc
    fl = np.zeros((128, 2), np.float32)
    fl[:, 0] = 1.0 if ch == 0 else 0.0
    fl[:, 1] = 0.0 if ch == 0 else 1.0
    m["flags"] = fl
    m["bias_meta_first"] = sh["_bmf"] if ch == 0 else sh["_bmr_full"]
    return m


_NC_CACHE = {}


def kernel(x, meta_tokens, rel_bias_table, norm_mix, norm_ffn, norm_final,
           attn_w_qkv, attn_b_qkv, attn_w_o, attn_b_o, attn_sinks,
           conv_w_in, conv_w, conv_w_out, pool_w, pool_scale,
           ffn_w_gate, ffn_w_up, ffn_w_down):
    inp = dict(meta_tokens=meta_tokens, rel_bias_table=rel_bias_table, norm_mix=norm_mix, norm_ffn=norm_ffn,
               norm_final=norm_final, attn_w_qkv=attn_w_qkv, attn_b_qkv=attn_b_qkv, attn_w_o=attn_w_o,
               attn_b_o=attn_b_o, attn_sinks=attn_sinks, conv_w_in=conv_w_in, conv_w=conv_w, conv_w_out=conv_w_out,
               pool_w=pool_w, pool_scale=pool_scale, ffn_w_gate=ffn_w_gate, ffn_w_up=ffn_w_up, ffn_w_down=ffn_w_down)
    x = np.ascontiguousarray(np.asarray(x, dtype=np.float32))
    sh = _prep_shared(inp)
    n_own = CHUNK // 128
    if "nc" not in _NC_CACHE:
        _NC_CACHE["nc"] = build(n_own=n_own)
    nc = _NC_CACHE["nc"]
    in_maps = [_core_inputs(x, sh, c, n_own) for c in range(N_CORES)]
    res = run_bass_kernel_spmd(nc, in_maps, core_ids=list(range(N_CORES)))
    out = np.empty((x.shape[0], SEQ, D), np.float32)
    for c in range(N_CORES):
        b, ch = divmod(c, SEQ // CHUNK)
        out[b, ch * CHUNK:(ch + 1) * CHUNK, :] = res.results[c]["out"]
    return out
```
